# Optimizing a Trainium2 kernel written in Bass

```python
import math
import jax, jax.numpy as jnp
from jax import lax
import numpy as np

D_MODEL = 1024
BATCH = 4
SEQ = 4096
DEPTH = 1
DEC_BATCH = 32
DEC_SEQ = 4
PAST_LEN = 16384
PAGE_SIZE = 128

MIX_W = D_MODEL
ATT_W = MIX_W // 2
HG_W = MIX_W - ATT_W
ATT_DH = 64
ATT_H = ATT_W // ATT_DH
DIL_PATTERNS = ((128, 1), (512, 4), (2048, 16))
MAX_WINDOW = 2048
Q_BLOCK = 128
HG_DK = 128
HG_H = HG_W // HG_DK
HG_DV = HG_W // HG_H
HG_CHUNK = 64
N_MEM = 256
X_H = 4
X_DH = 128
X_W = X_H * X_DH
D_FF = 2816
CONV_W = 3
EPS = 1e-6
IN_COLS = 3 * ATT_W + 4 * HG_W
IN_SPLITS = (ATT_W, 2 * ATT_W, 3 * ATT_W, 3 * ATT_W + HG_W, 3 * ATT_W + 2 * HG_W, 3 * ATT_W + 3 * HG_W)

kernel_name = 'hymba_hgrn2_dilated_decoder_step'


def rmsnorm(x, g):
    x32 = x.astype(jnp.float32)
    y = x32 * lax.rsqrt(jnp.mean(x32 * x32, axis=-1, keepdims=True) + EPS)
    return (y * g.astype(jnp.float32)).astype(x.dtype)


def head_rmsnorm(a, g):
    B, T, H, Dh = a.shape
    a32 = a.astype(jnp.float32)
    y = a32 * lax.rsqrt(jnp.mean(a32 * a32, axis=-1, keepdims=True) + EPS)
    return y.reshape(B, T, H * Dh) * g.astype(jnp.float32)


def alibi_slopes():
    return jnp.exp2(-8.0 * jnp.arange(1, ATT_H + 1, dtype=jnp.float32) / ATT_H)


def dilated_block(q, k_all, v_all, q_idx):
    slopes = alibi_slopes()
    lses, outs = [], []
    for win, dil in DIL_PATTERNS:
        dist = jnp.arange(win // dil + 1, dtype=jnp.int32) * dil
        idx = q_idx[:, None] - dist[None, :]
        valid = idx >= 0
        idx = jnp.maximum(idx, 0)
        kg = k_all[:, idx]
        vg = v_all[:, idx]
        s = jnp.einsum('bqhd,bqjhd->bhqj', q, kg).astype(jnp.float32) * (ATT_DH ** -0.5)
        s = s - slopes[:, None, None] * dist.astype(jnp.float32)
        s = jnp.where(valid[None, None], s, -jnp.inf)
        m = jnp.max(s, axis=-1, keepdims=True)
        p = jnp.exp(s - m)
        den = jnp.sum(p, axis=-1, keepdims=True)
        o = jnp.einsum('bhqj,bqjhd->bhqd', p.astype(vg.dtype), vg).astype(jnp.float32) / den
        lses.append(m + jnp.log(den))
        outs.append(o)
    w = jax.nn.softmax(jnp.stack(lses), axis=0)
    out = jnp.sum(w * jnp.stack(outs), axis=0)
    return out.transpose(0, 2, 1, 3).astype(q.dtype)


def dilated_attention(q, k_all, v_all, q_idx):
    B, T, H, Dh = q.shape
    if T % Q_BLOCK == 0 and T > Q_BLOCK:
        nb = T // Q_BLOCK
        qb = q.reshape(B, nb, Q_BLOCK, H, Dh).transpose(1, 0, 2, 3, 4)
        ib = q_idx.reshape(nb, Q_BLOCK)
        ob = lax.map(lambda a: dilated_block(a[0], k_all, v_all, a[1]), (qb, ib))
        return ob.transpose(1, 0, 2, 3, 4).reshape(B, T, H, Dh)
    return dilated_block(q, k_all, v_all, q_idx)


def hgrn2_recurrence(q, k, v, logf, s0):
    B, T, H, DK = q.shape
    DV = v.shape[-1]
    C = math.gcd(T, HG_CHUNK)
    n = T // C

    def to_chunks(a):
        return a.reshape(B, n, C, H, a.shape[-1]).transpose(1, 0, 3, 2, 4)

    causal = jnp.tril(jnp.ones((C, C), dtype=bool))[None, None, :, :, None]

    def step(S, inp):
        qc, kc, vc, gc = inp
        b = jnp.cumsum(gc, axis=2)
        decay = jnp.exp(jnp.where(causal, b[:, :, :, None, :] - b[:, :, None, :, :], -jnp.inf))
        a = jnp.einsum('bhtk,bhsk,bhtsk->bhts', qc, kc, decay)
        o = jnp.einsum('bhts,bhsv->bhtv', a, vc) + jnp.einsum('bhtk,bhkv->bhtv', qc * jnp.exp(b), S)
        b_last = b[:, :, -1, :]
        S = jnp.exp(b_last)[..., None] * S + jnp.einsum('bhsk,bhsv->bhkv', kc * jnp.exp(b_last[:, :, None, :] - b), vc)
        return S, o

    s_fin, o = lax.scan(step, s0.astype(jnp.float32), (to_chunks(q), to_chunks(k), to_chunks(v), to_chunks(logf)))
    o = o.transpose(1, 0, 3, 2, 4).reshape(B, T, H, DV)
    return o, s_fin


def hgrn_lower_bound(lb_logits, layer):
    probs = jax.nn.softmax(lb_logits.astype(jnp.float32), axis=0)
    return jnp.cumsum(probs, axis=0)[layer]


def memory_kv(mem, norm_mem, w_ck, w_cv):
    B = mem.shape[0]
    m = rmsnorm(mem, norm_mem)
    return ((m @ w_ck).reshape(B, N_MEM, X_H, X_DH), (m @ w_cv).reshape(B, N_MEM, X_H, X_DH))


def decoder_layer(x, k_past, v_past, hg_s0, conv_buf, mem_k, mem_v, lb,
                  norm_mix, w_in, att_out_norm, hg_out_norm, w_out,
                  norm_cross, w_cq, w_co, norm_ffn, w_gate, w_up, conv_w, conv_b, w_down):
    B, T, _ = x.shape
    P = k_past.shape[1]
    f32 = jnp.float32
    h = rmsnorm(x, norm_mix)
    z = h @ w_in
    aq, ak, av, hq, hf, hi, hg = jnp.split(z, IN_SPLITS, axis=-1)
    aq = aq.reshape(B, T, ATT_H, ATT_DH)
    ak = ak.reshape(B, T, ATT_H, ATT_DH)
    av = av.reshape(B, T, ATT_H, ATT_DH)
    k_all = jnp.concatenate([k_past.astype(ak.dtype), ak], axis=1)
    v_all = jnp.concatenate([v_past.astype(av.dtype), av], axis=1)
    q_idx = P + jnp.arange(T, dtype=jnp.int32)
    att = dilated_attention(aq, k_all, v_all, q_idx)
    att = head_rmsnorm(att, att_out_norm).astype(x.dtype)
    f = lb + (1.0 - lb) * jax.nn.sigmoid(hf.astype(f32))
    qh = jax.nn.silu(hq.astype(f32)) * (HG_DK ** -0.5)
    o, s_new = hgrn2_recurrence(qh.reshape(B, T, HG_H, HG_DK), (1.0 - f).reshape(B, T, HG_H, HG_DK),
                                hi.astype(f32).reshape(B, T, HG_H, HG_DV), jnp.log(f).reshape(B, T, HG_H, HG_DK), hg_s0)
    o = (head_rmsnorm(o, hg_out_norm) * jax.nn.silu(hg.astype(f32))).astype(x.dtype)
    x = x + jnp.concatenate([att, o], axis=-1) @ w_out
    c = rmsnorm(x, norm_cross)
    cq = (c @ w_cq).reshape(B, T, X_H, X_DH)
    s = jnp.einsum('bthd,bmhd->bhtm', cq, mem_k.astype(cq.dtype)).astype(f32) * (X_DH ** -0.5)
    pr = jax.nn.softmax(s, axis=-1)
    co = jnp.einsum('bhtm,bmhd->bthd', pr.astype(x.dtype), mem_v.astype(x.dtype)).reshape(B, T, X_W)
    x = x + co @ w_co
    u = rmsnorm(x, norm_ffn)
    ug = u @ w_gate
    up = jnp.concatenate([conv_buf.astype(ug.dtype), ug], axis=1)
    conv = conv_b + sum(conv_w[j] * up[:, j:j + T] for j in range(CONV_W))
    x = x + (jax.nn.silu(conv) * (u @ w_up)) @ w_down
    return x, ak, av, s_new, up[:, -(CONV_W - 1):]


def setup_inputs(seed: int = 0) -> dict:
    key = jax.random.key(seed)
    ks = iter(jax.random.split(key, 40))
    nrm = lambda shape, scale=1.0: jax.random.normal(next(ks), shape, jnp.float32) * scale
    gain = lambda shape: 1.0 + 0.02 * jax.random.normal(next(ks), shape, jnp.float32)
    win_buf = min(MAX_WINDOW, PAST_LEN)
    return {
        'x_prompt': nrm((BATCH, SEQ, D_MODEL)),
        'x_sample': nrm((DEC_BATCH, DEC_SEQ, D_MODEL)),
        'cache_win_k': nrm((DEPTH, DEC_BATCH, win_buf, ATT_H, ATT_DH)),
        'cache_win_v': nrm((DEPTH, DEC_BATCH, win_buf, ATT_H, ATT_DH)),
        'state_hgrn': nrm((DEPTH, DEC_BATCH, HG_H, HG_DK, HG_DV), 0.1),
        'state_ffn_conv': nrm((DEPTH, DEC_BATCH, CONV_W - 1, D_FF)),
        'cache_mem_k': nrm((DEPTH, DEC_BATCH, N_MEM, X_H, X_DH)),
        'cache_mem_v': nrm((DEPTH, DEC_BATCH, N_MEM, X_H, X_DH)),
        'mem_prompt': nrm((BATCH, N_MEM, D_MODEL)),
        'hg_lb_logits': nrm((DEPTH + 1, HG_W)),
        'norm_mix': gain((DEPTH, D_MODEL)),
        'w_in': nrm((DEPTH, D_MODEL, IN_COLS), D_MODEL ** -0.5),
        'att_out_norm': gain((DEPTH, ATT_W)),
        'hg_out_norm': gain((DEPTH, HG_W)),
        'w_out': nrm((DEPTH, MIX_W, D_MODEL), MIX_W ** -0.5),
        'norm_cross': gain((DEPTH, D_MODEL)),
        'norm_mem': gain((DEPTH, D_MODEL)),
        'w_cq': nrm((DEPTH, D_MODEL, X_W), D_MODEL ** -0.5),
        'w_ck': nrm((DEPTH, D_MODEL, X_W), D_MODEL ** -0.5),
        'w_cv': nrm((DEPTH, D_MODEL, X_W), D_MODEL ** -0.5),
        'w_co': nrm((DEPTH, X_W, D_MODEL), X_W ** -0.5),
        'norm_ffn': gain((DEPTH, D_MODEL)),
        'w_gate': nrm((DEPTH, D_MODEL, D_FF), D_MODEL ** -0.5),
        'w_up': nrm((DEPTH, D_MODEL, D_FF), D_MODEL ** -0.5),
        'conv_w': nrm((DEPTH, CONV_W, D_FF), CONV_W ** -0.5),
        'conv_b': nrm((DEPTH, D_FF), 0.01),
        'w_down': nrm((DEPTH, D_FF, D_MODEL), D_FF ** -0.5),
        'norm_final': gain((D_MODEL,)),
    }


def reference(x_prompt, x_sample, cache_win_k, cache_win_v, state_hgrn, state_ffn_conv, cache_mem_k, cache_mem_v,
              mem_prompt, hg_lb_logits, norm_mix, w_in, att_out_norm, hg_out_norm, w_out, norm_cross, norm_mem,
              w_cq, w_ck, w_cv, w_co, norm_ffn, w_gate, w_up, conv_w, conv_b, w_down, norm_final):
    Bp, T, _ = x_prompt.shape
    keep = min(MAX_WINDOW, T)
    yp, ys = x_prompt, x_sample
    pk, pv, ps, pc, pmk, pmv = [], [], [], [], [], []
    sk, sv, ss, sc = [], [], [], []
    for l in range(DEPTH):
        lb = hgrn_lower_bound(hg_lb_logits, l)
        weights = (norm_mix[l], w_in[l], att_out_norm[l], hg_out_norm[l], w_out[l], norm_cross[l], w_cq[l], w_co[l],
                   norm_ffn[l], w_gate[l], w_up[l], conv_w[l], conv_b[l], w_down[l])
        mk, mv = memory_kv(mem_prompt, norm_mem[l], w_ck[l], w_cv[l])
        empty = jnp.zeros((Bp, 0, ATT_H, ATT_DH), x_prompt.dtype)
        yp, k_new, v_new, s_new, c_new = decoder_layer(
            yp, empty, empty, jnp.zeros((Bp, HG_H, HG_DK, HG_DV), jnp.float32),
            jnp.zeros((Bp, CONV_W - 1, D_FF), x_prompt.dtype), mk, mv, lb, *weights)
        pk.append(k_new[:, T - keep:])
        pv.append(v_new[:, T - keep:])
        ps.append(s_new)
        pc.append(c_new)
        pmk.append(mk)
        pmv.append(mv)
        ys, k_new, v_new, s_new, c_new = decoder_layer(
            ys, cache_win_k[l], cache_win_v[l], state_hgrn[l], state_ffn_conv[l], cache_mem_k[l], cache_mem_v[l],
            lb, *weights)
        sk.append(k_new)
        sv.append(v_new)
        ss.append(s_new)
        sc.append(c_new)
    yp = rmsnorm(yp, norm_final)
    ys = rmsnorm(ys, norm_final)
    return (yp, ys, jnp.stack(pk), jnp.stack(pv), jnp.stack(ps), jnp.stack(pc), jnp.stack(pmk), jnp.stack(pmv),
            jnp.stack(sk), jnp.stack(sv), jnp.stack(ss), jnp.stack(sc))
```

```python
import contextlib
import os
import numpy as np
import concourse.bass as bass
import concourse.mybir as mybir
from concourse.bass_utils import run_bass_kernel_spmd

F32 = mybir.dt.float32
BF16 = mybir.dt.bfloat16
AF = mybir.ActivationFunctionType
ALU = mybir.AluOpType
AX = mybir.AxisListType

D = 1024
NT = 32
T0 = 15
NMAIN = NT - T0
NOUT = 16
DFF = 2816
NF = 22
EPS = 1e-6
SEM_LIMIT = 30000
NSQ = 4
NS = 16


class Buf:
    __slots__ = ("name", "w", "r", "x")

    def __init__(self, name="", x=False):
        self.name = name
        self.w = {}
        self.r = {}
        self.x = x


class Sched:
    def __init__(self, nc):
        self.nc = nc
        self.eng = {"pe": nc.tensor, "act": nc.scalar, "dve": nc.vector,
                    "pool": nc.gpsimd, "sp": nc.sync}
        self.sems = {}
        self.cnt = {}
        self.seen = {e: {} for e in self.eng}
        self.final = {}
        self.nsem = 0
        self.ninst = 0
        self.nwait = 0

    def _stream(self, name):
        ep, c = self.cnt.get(name, (0, 0))
        if c >= SEM_LIMIT:
            ep, c = ep + 1, 0
        key = "%s#%d" % (name, ep)
        if key not in self.sems:
            self.sems[key] = self.nc.alloc_semaphore("s%d_%s" % (self.nsem, name))
            self.nsem += 1
        return key, ep, c

    def _deps(self, reads, writes):
        deps = {}
        for b in reads:
            for k, v in b.w.items():
                if deps.get(k, 0) < v:
                    deps[k] = v
            if b.x:
                for k, v in b.r.items():
                    if deps.get(k, 0) < v:
                        deps[k] = v
        for b in writes:
            for d in (b.w, b.r):
                for k, v in d.items():
                    if deps.get(k, 0) < v:
                        deps[k] = v
        return deps

    def _wait(self, ename, deps):
        e = self.eng[ename]
        seen = self.seen[ename]
        for k, v in deps.items():
            if ename == "pe" and k.startswith("pe#"):
                continue
            if seen.get(k, 0) < v:
                e.wait_ge(self.sems[k], v)
                seen[k] = v
                self.nwait += 1

    def _commit(self, key, val, reads, writes):
        for b in reads:
            if b.r.get(key, 0) < val:
                b.r[key] = val
        for b in writes:
            b.w = {key: val}
            b.r = {}

    def op(self, ename, fn, reads=(), writes=()):
        self.group(ename, [fn], reads, writes)

    def group(self, ename, fns, reads=(), writes=()):
        self._wait(ename, self._deps(reads, writes))
        key, ep, c = self._stream(ename)
        ins = None
        for fn in fns:
            ins = fn(self.eng[ename])
            self.ninst += 1
        ins.then_inc(self.sems[key], 1)
        c += 1
        self.cnt[ename] = (ep, c)
        self._commit(key, c, reads, writes)

    def dma(self, qname, out, in_, reads=(), writes=(), sem="dma", **kw):
        self._wait(qname, self._deps(reads, writes))
        sname = "d_" + sem
        key, ep, c = self._stream(sname)
        ins = self.eng[qname].dma_start(out=out, in_=in_, **kw)
        ins.then_inc(self.sems[key], 16)
        c += 16
        self.cnt[sname] = (ep, c)
        self.final[key] = c
        self._commit(key, c, reads, writes)
        self.ninst += 1

    def seal(self, bufs, sem):
        sname = "d_" + sem
        key, ep, c = self._stream(sname)
        for b in bufs:
            b.w = {key: c}

    def barrier(self):
        allv = dict(self.final)
        for name, (ep, c) in self.cnt.items():
            if c > 0:
                allv["%s#%d" % (name, ep)] = c
        for e in self.eng:
            self._wait(e, allv)

    def finish(self):
        self._wait("sp", dict(self.final))


def mm(out, lhsT, rhs, start, stop, **kw):
    return lambda e: e.matmul(out, lhsT=lhsT, rhs=rhs, start=start, stop=stop, **kw)


def tr(out, in_, ident):
    return lambda e: e.transpose(out=out, in_=in_, identity=ident)


def _slopes():
    return np.exp2(-8.0 * np.arange(1, 9, dtype=np.float64) / 8.0)


def _wfun(delta, h):
    d = np.asarray(delta, dtype=np.int64)
    c = ((d >= 0) & (d <= 128)).astype(np.float64)
    c += ((d >= 0) & (d <= 512) & (d % 4 == 0))
    c += ((d >= 0) & (d <= 2048) & (d % 16 == 0))
    return c * np.exp(-_slopes()[h] * np.maximum(d, 0))


def _consts():
    cs = {}
    cs["ident"] = np.eye(128, dtype=np.float32)
    u = np.arange(128)[:, None]
    q = np.arange(128)[None, :]
    tab = np.zeros((128, 17, 8, 128), np.float32)
    for j in range(17):
        for h in range(8):
            tab[:, j, h, :] = _wfun(128 * j + q - u, h)
    cs["tab"] = tab
    ta = np.zeros((128, 4, 4, 8), np.float32)
    for blk in range(4):
        for i in range(4):
            for h in range(8):
                ta[:, blk, i, h] = _wfun(2048 + i - (1536 + 128 * blk + np.arange(128)), h)
    cs["ta"] = ta
    tb = np.zeros((96, 4, 8), np.float32)
    for i in range(4):
        for h in range(8):
            tb[:, i, h] = _wfun(2048 - 16 * np.arange(96), h)
    cs["tb"] = tb
    tn = np.zeros((16, 16, 8), np.float32)
    for uu in range(16):
        for qq in range(16):
            if uu // 4 == qq // 4 and uu <= qq:
                for h in range(8):
                    tn[uu, qq, h] = _wfun(qq - uu, h)
    cs["tn"] = tn
    for T, L, nm in ((128, 32, "p"), (16, 4, "s")):
        s = np.arange(T)
        same = (s[:, None] // L) == (s[None, :] // L)
        cs["bt" + nm] = (same & (s[:, None] <= s[None, :])).astype(np.float32)
        cs["m2" + nm] = (same & (s[:, None] > s[None, :])).astype(np.float32)
        cs["cm" + nm] = ((s[:, None] // L) == np.arange(4)[None, :]).astype(np.float32)
    selb = np.zeros((16, 16, 128), np.float32)
    for t in range(16):
        selb[t, t, :] = 1.0
    cs["selb"] = selb
    cs["onesm"] = np.full((128, 128), 1.0 / 128.0, np.float32)
    return cs


def build_nc(stop_after="F", debug=False):
    nc = bass.Bass("TRN2", target_bir_lowering=False)
    S = Sched(nc)

    def din(name, shape):
        return nc.dram_tensor(name, list(shape), F32, kind="ExternalInput").ap()

    def dout(name, shape):
        return nc.dram_tensor(name, list(shape), F32, kind="ExternalOutput").ap()

    xc = din("xc", (NT * 128, D))
    vflag = din("vflag", (128, NT))
    hflag = din("hflag", (128, 1))
    xs = din("xs", (NS, D))
    cwk = din("cwk", (NSQ, 2048, 512))
    cwv = din("cwv", (NSQ, 2048, 512))
    shg = din("shg", (NSQ, 4, 128, 128))
    sfc = din("sfc", (NSQ * 2, DFF))
    cmk = din("cmk", (NSQ, 256, 512))
    cmv = din("cmv", (NSQ, 256, 512))
    memp = din("memp", (256, D))
    lbl = din("hg_lb_logits", (2, 512))
    norm_mix = din("norm_mix", (D,))
    w_in = din("w_in", (D, 3584))
    att_out_norm = din("att_out_norm", (512,))
    hg_out_norm = din("hg_out_norm", (512,))
    w_out = din("w_out", (D, D))
    norm_cross = din("norm_cross", (D,))
    norm_mem = din("norm_mem", (D,))
    w_cq = din("w_cq", (D, 512))
    w_ck = din("w_ck", (D, 512))
    w_cv = din("w_cv", (D, 512))
    w_co = din("w_co", (512, D))
    norm_ffn = din("norm_ffn", (D,))
    w_gate = din("w_gate", (D, DFF))
    w_up = din("w_up", (D, DFF))
    conv_w = din("conv_w", (3, DFF))
    conv_b = din("conv_b", (DFF,))
    w_down = din("w_down", (DFF, D))
    norm_final = din("norm_final", (D,))
    c_ident = din("ident", (128, 128))
    c_tab = din("tab", (128, 17, 8, 128))
    c_ta = din("ta", (128, 4, 4, 8))
    c_tb = din("tb", (96, 4, 8))
    c_tn = din("tn", (16, 16, 8))
    c_btp = din("btp", (128, 128)); c_m2p = din("m2p", (128, 128)); c_cmp = din("cmp", (128, 4))
    c_bts = din("bts", (16, 16)); c_m2s = din("m2s", (16, 16)); c_cms = din("cms", (16, 4))
    c_selb = din("selb", (16, 16, 128))
    c_onesm = din("onesm", (128, 128))

    o_y = dout("o_y", (NOUT * 128, D))
    o_wk = dout("o_wk", (NOUT * 128, 512))
    o_wv = dout("o_wv", (NOUT * 128, 512))
    o_hs = dout("o_hs", (4, 128, 128))
    o_fc = dout("o_fc", (2, DFF))
    o_mk = dout("o_mk", (256, 512))
    o_mv = dout("o_mv", (256, 512))
    o_ys = dout("o_ys", (NS, D))
    o_wks = dout("o_wks", (NS, 512))
    o_wvs = dout("o_wvs", (NS, 512))
    o_hss = dout("o_hss", (NSQ, 4, 128, 128))
    o_fcs = dout("o_fcs", (NS, DFF))
    x2d = nc.dram_tensor("d_x2" if debug else "x2d", [(NMAIN) * 128 + NS, D], F32,
                          kind="ExternalOutput" if debug else "Internal").ap()
    Bx2d = [Buf() for _ in range(NMAIN + 1)]
    Bout = Buf("out")

    es_all = contextlib.ExitStack()

    SB_RESERVE = 229376 - 212992

    def sbt(es, name, shape, dt=F32):
        a = es.enter_context(nc.sbuf_tensor("sb_" + name, list(shape), dt)).ap()
        assert nc.sbuf_bytes_remaining >= SB_RESERVE, ("SBUF budget exceeded at", name, nc.sbuf_bytes_remaining)
        return a

    banks = [nc.alloc_psum_tensor("bank%d" % i, [128, 512], F32).ap() for i in range(8)]
    Bbank = [Buf("bank%d" % i, x=True) for i in range(8)]

    class Rot:
        def __init__(self, ids):
            self.ids = ids
            self.i = 0

        def get(self):
            b = self.ids[self.i % len(self.ids)]
            self.i += 1
            return banks[b], Bbank[b]

    ident = sbt(es_all, "ident", (128, 128)); Bc = Buf("const")
    gcols = sbt(es_all, "gcols", (128, 40))
    cwc = sbt(es_all, "cwc", (128, NF, 3))
    cbc = sbt(es_all, "cbc", (128, NF))
    lbc = sbt(es_all, "lbc", (128, 2, 4))
    vfl = sbt(es_all, "vfl", (128, NT))
    hfl = sbt(es_all, "hfl", (128, 1))
    onesm = sbt(es_all, "onesm", (128, 128))
    onesb = sbt(es_all, "onesb", (128, 128), BF16)
    btp = sbt(es_all, "btp", (128, 128)); m2p = sbt(es_all, "m2p", (128, 128)); cmp_ = sbt(es_all, "cmp", (128, 4))
    bts = sbt(es_all, "bts", (16, 16)); m2s = sbt(es_all, "m2s", (16, 16)); cms = sbt(es_all, "cms", (16, 4))
    mixT = sbt(es_all, "mixT", (128, 4, NMAIN * 128), BF16)
    BmixA = [Buf() for _ in range(NMAIN)]
    BmixH = [Buf() for _ in range(NMAIN)]
    mixTs = sbt(es_all, "mixTs", (128, 4, NS), BF16)
    BmixsA = Buf(); BmixsH = Buf()
    junk = sbt(es_all, "junk", (128, D), BF16); Bjunk = Buf()
    stat = sbt(es_all, "stat", (128, 8)); Bstat = Buf()
    xn = sbt(es_all, "xn", (128, D), BF16); Bxn = Buf()
    identb = sbt(es_all, "identb", (128, 128), BF16)
    anb = sbt(es_all, "anb", (128, 256), BF16); Banb = Buf()

    cq = "c0"
    S.dma("sp", ident, c_ident, writes=[Bc], sem=cq)
    for i, g in enumerate((norm_mix, norm_cross, norm_ffn, norm_mem)):
        S.dma("sp", gcols[:, 8 * i:8 * i + 8], g.rearrange("(k p) -> p k", p=128), writes=[Bc], sem=cq,
              allow_slow_non_contiguous=True)
    S.dma("sp", gcols[:, 32:36], att_out_norm.rearrange("(k p) -> p k", p=128), writes=[Bc], sem=cq,
          allow_slow_non_contiguous=True)
    S.dma("sp", gcols[:, 36:40], hg_out_norm.rearrange("(k p) -> p k", p=128), writes=[Bc], sem=cq,
          allow_slow_non_contiguous=True)
    for j in range(3):
        S.dma("sp", cwc[:, :, j], conv_w[j].rearrange("(f p) -> p f", p=128), writes=[Bc], sem=cq,
              allow_slow_non_contiguous=True)
    S.dma("sp", cbc, conv_b.rearrange("(f p) -> p f", p=128), writes=[Bc], sem=cq, allow_slow_non_contiguous=True)
    S.dma("sp", lbc, lbl.rearrange("l (h p) -> p l h", p=128), writes=[Bc], sem=cq, allow_slow_non_contiguous=True)
    S.dma("sp", vfl, vflag, writes=[Bc], sem=cq)
    S.dma("sp", hfl, hflag, writes=[Bc], sem=cq)
    S.dma("sp", onesm, c_onesm, writes=[Bc], sem=cq)
    for dst, src in ((btp, c_btp), (m2p, c_m2p), (cmp_, c_cmp), (bts, c_bts), (m2s, c_m2s), (cms, c_cms)):
        S.dma("sp", dst, src, writes=[Bc], sem=cq)
    S.seal([Bc], cq)
    Blb = Buf("lb")
    def lb_compute(t_):
        S.op("dve", lambda e, t_=t_: e.tensor_tensor(out=t_[:, 0], in0=t_[:, 0], in1=t_[:, 1], op=ALU.subtract),
             reads=[Bc], writes=[Blb])
        S.op("act", lambda e, t_=t_: e.activation(out=t_[:, 1], in_=t_[:, 0], func=AF.Sigmoid, scale=-1.0),
             reads=[Blb], writes=[Blb])
        S.op("act", lambda e, t_=t_: e.activation(out=t_[:, 0], in_=t_[:, 0], func=AF.Sigmoid),
             reads=[Blb], writes=[Blb])

    lb_compute(lbc)
    Bones = Buf()
    S.op("dve", lambda e: e.memset(onesb, 1.0), writes=[Bones])
    S.op("dve", lambda e: e.tensor_copy(out=identb, in_=ident), reads=[Bc], writes=[Bones])

    wrot = Rot([0, 1])

    def wload(dst, src_rows, ncols, c0, bufs, sem, nk=8):
        v = src_rows.rearrange("(k p) c -> p k c", p=128)
        for k in range(nk):
            S.dma("pool", dst[:, k, :], v[:, k, c0:c0 + ncols], writes=bufs, sem=sem)

    def rstd_from_ss(T, col_in, col_out, n):
        S.op("act", lambda e: e.activation(out=stat[:T, col_out], in_=stat[:T, col_in], func=AF.Ln,
                                           scale=1.0 / n, bias=EPS), reads=[Bstat], writes=[Bstat])
        S.op("act", lambda e: e.activation(out=stat[:T, col_out], in_=stat[:T, col_out], func=AF.Exp,
                                           scale=-0.5), reads=[Bstat], writes=[Bstat])

    def norm_T(xa, Bx, T, gc0, dstT, BdT, rot=None):
        S.op("act", lambda e: e.activation(out=junk[:T], in_=xa, func=AF.Square, accum_out=stat[:T, 0:1]),
             reads=[Bx], writes=[Bjunk, Bstat])
        rstd_from_ss(T, slice(0, 1), slice(1, 2), float(D))
        S.op("pool", lambda e: e.tensor_scalar(out=xn[:T], in0=xa, scalar1=stat[:T, 1:2], scalar2=1.0,
                                               op0=ALU.mult, op1=ALU.mult), reads=[Bx, Bstat], writes=[Bxn])
        bk, Bb = (rot or wrot).get()
        bv = bk.bitcast(BF16).rearrange("p (a b) -> p a b", a=8)
        S.group("pe", [tr(bv[:, k, :T], xn[:T, k * 128:(k + 1) * 128], identb[:T, :T]) for k in range(8)],
                reads=[Bxn, Bones], writes=[Bb])
        S.op("dve", lambda e: e.tensor_tensor(
            out=dstT[:, 0:8, :T], in0=bv[:, :, :T],
            in1=gcols[:, gc0:gc0 + 8].unsqueeze(2).to_broadcast([128, 8, T]), op=ALU.mult),
            reads=[Bb, Bc], writes=[BdT])

    def proj_fm(bv, Bb, W, c0, nchunk, rhsT, T, Brd):
        fns = []
        for c in range(nchunk):
            for k in range(8):
                fns.append(mm(bv[:, c, :T], W[:, k, c0 + c * 128:c0 + (c + 1) * 128], rhsT[:, k, :T], k == 0, k == 7))
        S.group("pe", fns, reads=Brd, writes=[Bb])

    def proj_tm(bk, Bb, W, c0, ncol, lhsT, T, Brd, nk=8):
        fns = [mm(bk[:T, :ncol], lhsT[:, k, :T], W[:, k, c0:c0 + ncol], k == 0, k == nk - 1) for k in range(nk)]
        S.group("pe", fns, reads=Brd, writes=[Bb])

    def headnorm_tm(T, o_sb, Bo, nh, dh, dst, Bdst, es):
        sq = hn_sq[:T, :nh * dh].rearrange("p (h d) -> p h d", h=nh)
        S.op("pool", lambda e: e.tensor_tensor(out=sq, in0=o_sb, in1=o_sb, op=ALU.mult), reads=[Bo], writes=[Bhnsq])
        S.op("dve", lambda e: e.tensor_reduce(out=stat8[:T, 0:nh], in_=sq, axis=AX.X, op=ALU.add),
             reads=[Bhnsq], writes=[Bstat8])
        S.op("act", lambda e: e.activation(out=stat8[:T, 0:nh], in_=stat8[:T, 0:nh], func=AF.Ln, scale=1.0 / dh,
                                           bias=EPS), reads=[Bstat8], writes=[Bstat8])
        S.op("act", lambda e: e.activation(out=stat8[:T, 0:nh], in_=stat8[:T, 0:nh], func=AF.Exp, scale=-0.5),
             reads=[Bstat8], writes=[Bstat8])
        S.op("dve", lambda e: e.tensor_tensor(out=dst, in0=o_sb,
                                              in1=stat8[:T, 0:nh].unsqueeze(2).to_broadcast([T, nh, dh]),
                                              op=ALU.mult), reads=[Bo, Bstat8], writes=[Bdst])

    hn_sq = sbt(es_all, "hn_sq", (128, 512)); Bhnsq = Buf()
    stat8 = sbt(es_all, "stat8", (128, 16)); Bstat8 = Buf()
    on_sb = sbt(es_all, "on_sb", (128, 512)); Bon = Buf()
    an_sb = sbt(es_all, "an_sb", (128, 512)); Ban = Buf()

    def attn_finish(T, obanks, Bob, nh, dh, gc0, dstT, BdT, par=False):
        hh = nh // 2
        for g in range(2):
            S.op("dve", lambda e, g=g: e.tensor_scalar(out=stat8[:T, 8 + g * hh:8 + (g + 1) * hh],
                                                       in0=obanks[g][:T, :, dh], scalar1=1e-30, scalar2=None,
                                                       op0=ALU.add), reads=[Bob[g]], writes=[Bstat8])
        S.op("dve", lambda e: e.reciprocal(out=stat8[:T, 8:8 + nh], in_=stat8[:T, 8:8 + nh]),
             reads=[Bstat8], writes=[Bstat8])
        onv = on_sb[:T, :nh * dh].rearrange("p (h d) -> p h d", h=nh)
        for g in range(2):
            S.op("dve", lambda e, g=g: e.tensor_tensor(
                out=(onv[:, g:nh:2, :] if par else onv[:, g * hh:(g + 1) * hh, :]), in0=obanks[g][:T, :, 0:dh],
                in1=stat8[:T, 8 + g * hh:8 + (g + 1) * hh].unsqueeze(2).to_broadcast([T, hh, dh]), op=ALU.mult),
                reads=[Bob[g], Bstat8], writes=[Bon])
        return onv

    def tm_to_fm(T, src, Bsrc, nchunk, gc0, dstT, BdT, dt_scale=True):
        bk, Bb = wrot.get()
        bv = bk.rearrange("p (a b) -> p a b", a=4)
        S.group("pe", [tr(bv[:, c, :T], src[:T, c * 128:(c + 1) * 128], ident[:T, :T]) for c in range(nchunk)],
                reads=[Bsrc, Bc], writes=[Bb])
        if gc0 is None:
            S.op("act", lambda e: e.activation(out=dstT[:, 0:nchunk, :T], in_=bv[:, 0:nchunk, :T], func=AF.Copy),
                 reads=[Bb], writes=[BdT])
        else:
            S.op("dve", lambda e: e.tensor_tensor(
                out=dstT[:, 0:nchunk, :T], in0=bv[:, 0:nchunk, :T],
                in1=gcols[:, gc0:gc0 + nchunk].unsqueeze(2).to_broadcast([128, nchunk, T]), op=ALU.mult),
                reads=[Bb, Bc], writes=[BdT])

    esA = contextlib.ExitStack()
    Wqkv = sbt(esA, "Wqkv", (128, 8, 1536), BF16); BWqkv = Buf()
    wload(Wqkv, w_in, 1536, 0, [BWqkv], "wA")
    S.seal([BWqkv], "wA")
    xt = [sbt(esA, "xtA%d" % i, (128, D)) for i in range(2)]; Bxt = [Buf(), Buf()]
    hT = [sbt(esA, "hTA%d" % i, (128, 8, 128), BF16) for i in range(2)]; BhT = [Buf(), Buf()]

    esA1 = contextlib.ExitStack()
    kst = [sbt(esA1, "kst0", (128, 512))] * 2; Bkst = [Buf()] * 2
    vst = [sbt(esA1, "vst0", (128, 512))] * 2; Bvst = [Buf()] * 2
    KT = sbt(esA1, "KT", (128, 4, NT * 128), BF16); BKT = [Buf() for _ in range(NT)]
    V = sbt(esA1, "V", (128, NT, 8, 65), BF16); BV = [Buf() for _ in range(NT)]
    tab = sbt(esA1, "tab", (128, 17, 8, 128), BF16); Btab = Buf()
    for j in range(17):
        S.dma("pool", tab[:, j], c_tab[:, j], writes=[Btab], sem="tab")
    S.seal([Btab], "tab")
    QTg = [sbt(esA1, "QTg0", (128, 4, 512), BF16)] * 2; BQTg = [Buf()] * 2
    DEPTH = 4
    E = [sbt(esA1, "E%d" % i, (128, 512), BF16) for i in range(DEPTH)]; BE = [Buf() for _ in range(DEPTH)]
    PT = [sbt(esA1, "PT%d" % i, (128, 512), BF16) for i in range(DEPTH)]; BPT = [Buf() for _ in range(DEPTH)]
    osb = [sbt(esA1, "osb%d" % i, (128, 512)) for i in range(4)]; Bosb = [Buf() for _ in range(4)]
    wrot.ids = [0, 1, 2, 3]
    srot = wrot
    otb = [banks[4 + i] for i in range(4)]; Botb = [Bbank[4 + i] for i in range(4)]
    S.dma("sp", xt[0], xc[0:128, :], writes=[Bxt[0]], sem="xA0")

    def tileA(ti, qdst, Bq):
        sl = ti % 2
        outt = ti > T0
        if ti + 1 < NT:
            S.dma("sp", xt[1 - sl], xc[(ti + 1) * 128:(ti + 2) * 128, :], writes=[Bxt[1 - sl]], sem="xA%d" % (1 - sl))
        norm_T(xt[sl], Bxt[sl], 128, 0, hT[sl], BhT[sl])
        bk, Bb = wrot.get(); bv = bk.rearrange("p (a b) -> p a b", a=4)
        proj_fm(bv, Bb, Wqkv, 512, 4, hT[sl], 128, [BWqkv, BhT[sl]])
        S.op("act", lambda e: e.activation(out=KT[:, :, ti * 128:(ti + 1) * 128], in_=bv, func=AF.Copy),
             reads=[Bb], writes=[BKT[ti]])
        bk, Bb = wrot.get()
        proj_tm(bk, Bb, Wqkv, 1024, 512, hT[sl], 128, [BWqkv, BhT[sl]])
        S.op("dve", lambda e: e.tensor_copy(out=V[:, ti, :, 0:64], in_=bk.rearrange("p (h d) -> p h d", h=8)),
             reads=[Bb], writes=[BV[ti]])
        S.op("pool", lambda e: e.tensor_copy(out=V[:, ti, :, 64:65],
                                             in_=vfl[:, ti:ti + 1].unsqueeze(1).to_broadcast([128, 8, 1])),
             reads=[Bc], writes=[BV[ti]])
        if outt:
            so = ti % 2
            S.op("act", lambda e: e.activation(out=vst[so], in_=bk, func=AF.Copy), reads=[Bb], writes=[Bvst[so]])
            S.dma("sp", o_wv[(ti - 16) * 128:(ti - 15) * 128, :], vst[so], reads=[Bvst[so]], sem="ov")
            bk2, Bb2 = wrot.get()
            proj_tm(bk2, Bb2, Wqkv, 512, 512, hT[sl], 128, [BWqkv, BhT[sl]])
            S.op("act", lambda e: e.activation(out=kst[so], in_=bk2, func=AF.Copy), reads=[Bb2], writes=[Bkst[so]])
            S.dma("sp", o_wk[(ti - 16) * 128:(ti - 15) * 128, :], kst[so], reads=[Bkst[so]], sem="ok")
        if qdst is not None:
            bk3, Bb3 = wrot.get(); bv3 = bk3.rearrange("p (a b) -> p a b", a=4)
            proj_fm(bv3, Bb3, Wqkv, 0, 4, hT[sl], 128, [BWqkv, BhT[sl]])
            S.op("act", lambda e: e.activation(out=qdst, in_=bv3, func=AF.Copy), reads=[Bb3], writes=[Bq])

    def finish4(T, bank3, Bbk, hg, dstT, BdT):
        S.op("dve", lambda e: e.tensor_scalar(out=stat8[:T, 8:12], in0=bank3[:T, :, 64], scalar1=1e-30, scalar2=None,
                                              op0=ALU.add), reads=[Bbk], writes=[Bstat8])
        S.op("dve", lambda e: e.reciprocal(out=stat8[:T, 8:12], in_=stat8[:T, 8:12]), reads=[Bstat8], writes=[Bstat8])
        onv = on_sb[:T, 0:256].rearrange("p (h d) -> p h d", h=4)
        S.op("dve", lambda e: e.tensor_tensor(out=onv, in0=bank3[:T, :, 0:64],
                                              in1=stat8[:T, 8:12].unsqueeze(2).to_broadcast([T, 4, 64]), op=ALU.mult),
             reads=[Bbk, Bstat8], writes=[Bon])
        anv = anb[:T, 0:256].rearrange("p (h d) -> p h d", h=4)
        headnorm_tm(T, onv, Bon, 4, 64, anv, Banb, None)
        bk, Bb = wrot.get(); bv = bk.bitcast(BF16).rearrange("p (a b) -> p a b", a=8)
        S.group("pe", [tr(bv[:, c, :T], anb[:T, c * 128:(c + 1) * 128], identb[:T, :T]) for c in range(2)],
                reads=[Banb, Bones], writes=[Bb])
        S.op("dve", lambda e: e.tensor_tensor(
            out=dstT[:, 2 * hg:2 * hg + 2, :T], in0=bv[:, 0:2, :T],
            in1=gcols[:, 32 + 2 * hg:34 + 2 * hg].unsqueeze(2).to_broadcast([128, 2, T]), op=ALU.mult),
            reads=[Bb, Bc], writes=[BdT])

    if stop_after == "C":
        S.finish()
        return nc, S
    for ti in range(T0):
        tileA(ti, None, None)
    groups = [[T0]] + [list(range(t, t + 4)) for t in range(T0 + 1, NT, 4)]
    step = 0
    for gi_, grp in enumerate(groups):
        qs = gi_ % 2
        nt = len(grp); t0 = grp[0]; N = nt * 128
        for m, ti in enumerate(grp):
            tileA(ti, QTg[qs][:, :, m * 128:(m + 1) * 128], BQTg[qs])
        kb_lo = max(0, t0 - 16); kb_hi = t0 + nt - 1
        for hg in range(2):
            firstw = [True] * 4
            steps = []
            for kb in range(kb_lo, kb_hi + 1):
                m_lo = max(0, kb - t0); m_hi = min(nt - 1, kb + 16 - t0)
                for hh in range(4):
                    steps.append((kb, hh, m_lo, m_hi))
            sbanks = {}

            def emit_S(n):
                kb, hh, m_lo, m_hi = steps[n]
                h = 4 * hg + hh; c = h // 2; po = (h % 2) * 64
                c0_, c1_ = m_lo * 128, (m_hi + 1) * 128
                sbk, Bs = srot.get()
                sbanks[n] = (sbk, Bs)
                S.group("pe", [mm(sbk[:, c0_:c1_], KT[po:po + 64, c, kb * 128:(kb + 1) * 128],
                                  QTg[qs][po:po + 64, c, c0_:c1_], True, True)],
                        reads=[BKT[kb], BQTg[qs]], writes=[Bs])

            def emit_rest(n):
                kb, hh, m_lo, m_hi = steps[n]
                h = 4 * hg + hh
                nm = m_hi - m_lo + 1
                c0_, c1_ = m_lo * 128, (m_hi + 1) * 128
                j_lo = t0 + m_lo - kb
                es_ = n % DEPTH
                sbk, Bs = sbanks.pop(n)
                S.op("act", lambda e: e.activation(out=E[es_][:, c0_:c1_], in_=sbk[:, c0_:c1_], func=AF.Exp, scale=0.125),
                     reads=[Bs], writes=[BE[es_]])
                S.op("dve",
                     lambda e: e.tensor_tensor(out=PT[es_][:, c0_:c1_].rearrange("p (m q) -> p m q", m=nm),
                                               in0=E[es_][:, c0_:c1_].rearrange("p (m q) -> p m q", m=nm),
                                               in1=tab[:, j_lo:j_lo + nm, h, :], op=ALU.mult),
                     reads=[BE[es_], Btab], writes=[BPT[es_]])
                S.group("pe", [mm(otb[hh][:65, c0_:c1_], V[:, kb, h, :], PT[es_][:, c0_:c1_], firstw[hh],
                                  kb == kb_hi, skip_group_check=True)],
                        reads=[BPT[es_], BV[kb]], writes=[Botb[hh]])
                firstw[hh] = False

            ns_ = len(steps)
            for n in range(min(2, ns_)):
                emit_S(n)
            for p in range(0, ns_, 2):
                for n in (p + 2, p + 3):
                    if n < ns_:
                        emit_S(n)
                for n in (p, p + 1):
                    if n < ns_:
                        emit_rest(n)
            for hh in range(4):
                S.op("act", lambda e, hh=hh: e.activation(out=osb[hh][:65, :N], in_=otb[hh][:65, :N], func=AF.Copy),
                     reads=[Botb[hh]], writes=[Bosb[hh]])
            for m, ti in enumerate(grp):
                bk, Bb = srot.get()
                b3 = bk[:, 0:260].rearrange("p (h d) -> p h d", h=4)
                S.group("pe", [tr(b3[:, hh, :], osb[hh][:65, m * 128:(m + 1) * 128], ident[:65, :65]) for hh in range(4)],
                        reads=Bosb + [Bc], writes=[Bb])
                mt = ti - T0
                finish4(128, b3, Bb, hg, mixT[:, :, mt * 128:(mt + 1) * 128], BmixA[mt])
    S.barrier()
    esA1.close()
    zs = sbt(esA, "zs", (NS, 1536)); Bzs = Buf()
    hsT = sbt(esA, "hsT", (128, 8, NS), BF16); BhsT = Buf()
    xst = sbt(esA, "xst", (NS, D)); Bxst = Buf()
    S.dma("sp", xst, xs, writes=[Bxst], sem="xs")
    norm_T(xst, Bxst, NS, 0, hsT, BhsT)
    for n in range(3):
        bk, Bb = wrot.get()
        proj_tm(bk, Bb, Wqkv, n * 512, 512, hsT, NS, [BWqkv, BhsT])
        S.op("act", lambda e, bk=bk, n=n: e.activation(out=zs[:, n * 512:(n + 1) * 512], in_=bk[:NS, :], func=AF.Copy),
             reads=[Bb], writes=[Bzs])
    S.dma("pool", o_wks, zs[:, 512:1024], reads=[Bzs], sem="osmall")
    S.dma("pool", o_wvs, zs[:, 1024:1536], reads=[Bzs], sem="osmall")

    if stop_after == "A1":
        S.finish()
        return nc, S

    esA2 = contextlib.ExitStack()
    ta = sbt(esA2, "ta", (128, 4, 4, 8)); tb_ = sbt(esA2, "tb", (96, 4, 8)); tn = sbt(esA2, "tn", (16, 16, 8))
    selb = sbt(esA2, "selb", (16, 16, 128))
    Bt2 = Buf()
    S.dma("sp", selb, c_selb, writes=[Bt2], sem="c2")
    S.dma("sp", ta, c_ta, writes=[Bt2], sem="c2"); S.dma("sp", tb_, c_tb, writes=[Bt2], sem="c2")
    S.dma("sp", tn, c_tn, writes=[Bt2], sem="c2"); S.seal([Bt2], "c2")
    KA = [sbt(esA2, "KA%d" % i, (128, 4, 512)) for i in range(2)]
    VA = [sbt(esA2, "VA%d" % i, (128, 4, 8, 65)) for i in range(2)]
    KB = [sbt(esA2, "KB%d" % i, (96, 4, 512)) for i in range(2)]
    VB = [sbt(esA2, "VB%d" % i, (96, 4, 8, 65)) for i in range(2)]
    BKA = [Buf(), Buf()]; BVA = [Buf(), Buf()]; BKB = [Buf(), Buf()]; BVB = [Buf(), Buf()]
    for i in range(2):
        S.op("pool", lambda e, i=i: e.memset(VA[i], 1.0), writes=[BVA[i]])
        S.op("pool", lambda e, i=i: e.memset(VB[i], 1.0), writes=[BVB[i]])
    Vn = sbt(esA2, "Vn", (NS, 8, 65)); BVn = Buf()
    S.op("pool", lambda e: e.memset(Vn, 1.0), writes=[BVn])
    S.op("dve", lambda e: e.tensor_copy(out=Vn[:, :, 0:64], in_=zs[:, 1024:1536].rearrange("p (h d) -> p h d", h=8)),
         reads=[Bzs], writes=[BVn])
    prodA = sbt(esA2, "prodA", (128, 4, 512)); BprodA = Buf()
    prodB = sbt(esA2, "prodB", (128, 512)); BprodB = Buf()
    SCA = sbt(esA2, "SCA", (128, 4, 4, 8)); BSCA = Buf()
    SCB = sbt(esA2, "SCB", (96, 4, 8)); BSCB = Buf()
    SCN = sbt(esA2, "SCN", (NS, NS, 8)); BSCN = Buf()
    PAz = [sbt(esA2, "PAz%d" % i, (128, 4, NS, 8)) for i in range(NSQ)]; BPAz = [Buf() for _ in range(NSQ)]
    PBz = [sbt(esA2, "PBz%d" % i, (96, 4, NS, 8)) for i in range(NSQ)]; BPBz = [Buf() for _ in range(NSQ)]
    PN = sbt(esA2, "PN", (NS, NS, 8)); BPN = Buf()
    for i in range(NSQ):
        S.op("pool", lambda e, i=i: e.memset(PAz[i], 0.0), writes=[BPAz[i]])
        S.op("pool", lambda e, i=i: e.memset(PBz[i], 0.0), writes=[BPBz[i]])
    Bob = [Bbank[6], Bbank[7]]
    obs = [banks[6][:NS, 0:260].rearrange("p (h d) -> p h d", h=4),
           banks[7][:NS, 0:260].rearrange("p (h d) -> p h d", h=4)]
    first = [True, True]

    def pv(lhsT, rhs_of_h, Brd):
        for g in range(2):
            fns = []
            for hh in range(4):
                fns.append(mm(obs[g][:, hh, :], lhsT(4 * g + hh), rhs_of_h(4 * g + hh), first[g] and hh == 0, False,
                              skip_group_check=True))
            first[g] = False
            S.group("pe", fns, reads=Brd, writes=[Bob[g]])

    for s in range(NSQ):
        sl = s % 2
        S.dma("sp", KA[sl], cwk[s, 1536:2048, :].rearrange("(b p) c -> p b c", p=128), writes=[BKA[sl]], sem="ka%d" % sl)
        for blk in range(4):
            S.dma("sp", VA[sl][:, blk, :, 0:64],
                  cwv[s, 1536 + 128 * blk:1664 + 128 * blk, :].rearrange("p (h d) -> p h d", h=8),
                  writes=[BVA[sl]], sem="va%d" % sl)
        S.dma("sp", KB[sl], cwk[s, 0:1536, :].rearrange("(a r) c -> a r c", r=16)[:, 0:4, :], writes=[BKB[sl]],
              sem="kb%d" % sl)
        for r in range(4):
            S.dma("sp", VB[sl][:, r, :, 0:64],
                  cwv[s, 0:1536, :].rearrange("(a r) (h d) -> a r h d", r=16, h=8)[:, r], writes=[BVB[sl]],
                  sem="vb%d" % sl)
        for i in range(4):
            tq = 4 * s + i
            bk, Bb = wrot.get()
            S.group("pe", [mm(bk[:, :], selb[:, tq, :], zs[:, 0:512], True, True)], reads=[Bt2, Bzs], writes=[Bb])
            S.op("dve", lambda e, bk=bk, sl=sl: e.tensor_tensor(
                out=prodA, in0=KA[sl], in1=bk.unsqueeze(1).to_broadcast([128, 4, 512]), op=ALU.mult),
                reads=[BKA[sl], Bb], writes=[BprodA])
            S.op("dve", lambda e, i=i: e.tensor_reduce(
                out=SCA[:, :, i, :], in_=prodA.rearrange("p b (h d) -> p b h d", h=8), axis=AX.X, op=ALU.add),
                reads=[BprodA], writes=[BSCA])
            S.op("dve", lambda e, bk=bk, sl=sl, i=i: e.tensor_tensor(
                out=prodB[:96], in0=KB[sl][:, i, :], in1=bk[:96, :], op=ALU.mult),
                reads=[BKB[sl], Bb], writes=[BprodB])
            S.op("dve", lambda e, i=i: e.tensor_reduce(
                out=SCB[:, i, :], in_=prodB[:96].rearrange("p (h d) -> p h d", h=8), axis=AX.X, op=ALU.add),
                reads=[BprodB], writes=[BSCB])
            S.op("dve", lambda e, bk=bk: e.tensor_tensor(
                out=prodB[:NS], in0=zs[:, 512:1024], in1=bk[:NS, :], op=ALU.mult),
                reads=[Bzs, Bb], writes=[BprodB])
            S.op("dve", lambda e, tq=tq: e.tensor_reduce(
                out=SCN[:, tq, :], in_=prodB[:NS].rearrange("p (h d) -> p h d", h=8), axis=AX.X, op=ALU.add),
                reads=[BprodB], writes=[BSCN])
        S.op("act", lambda e: e.activation(out=SCA, in_=SCA, func=AF.Exp, scale=0.125), reads=[BSCA], writes=[BSCA])
        S.op("dve", lambda e, s=s: e.tensor_tensor(out=PAz[s][:, :, 4 * s:4 * s + 4, :], in0=SCA, in1=ta, op=ALU.mult),
             reads=[BSCA, Bt2], writes=[BPAz[s]])
        S.op("act", lambda e: e.activation(out=SCB, in_=SCB, func=AF.Exp, scale=0.125), reads=[BSCB], writes=[BSCB])
        for i in range(4):
            S.op("dve", lambda e, s=s, i=i: e.tensor_tensor(out=PBz[s][:, i, 4 * s + i, :], in0=SCB[:, i, :],
                                                            in1=tb_[:, i, :], op=ALU.mult),
                 reads=[BSCB, Bt2], writes=[BPBz[s]])
        for blk in range(4):
            pv(lambda h, blk=blk, s=s: PAz[s][:, blk, :, h], lambda h, blk=blk, sl=sl: VA[sl][:, blk, h, :],
               [BPAz[s], BVA[sl]])
            pv(lambda h, blk=blk, s=s: PBz[s][:, blk, :, h], lambda h, blk=blk, sl=sl: VB[sl][:, blk, h, :],
               [BPBz[s], BVB[sl]])
    S.op("act", lambda e: e.activation(out=SCN, in_=SCN, func=AF.Exp, scale=0.125), reads=[BSCN], writes=[BSCN])
    S.op("dve", lambda e: e.tensor_tensor(out=PN, in0=SCN, in1=tn, op=ALU.mult), reads=[BSCN, Bt2], writes=[BPN])
    pv(lambda h: PN[:, :, h], lambda h: Vn[:, h, :], [BPN, BVn])
    onv = attn_finish(NS, obs, Bob, 8, 64, 32, None, None)
    anv = an_sb[:NS, :].rearrange("p (h d) -> p h d", h=8)
    headnorm_tm(NS, onv, Bon, 8, 64, anv, Ban, None)
    tm_to_fm(NS, an_sb, Ban, 4, 32, mixTs[:, 0:4, :], BmixsA)
    S.barrier()
    esA2.close()
    esA.close()

    if debug:
        d_mix = nc.dram_tensor("d_mix", [128, 4, NMAIN * 128], BF16, kind="ExternalOutput").ap()
        d_mixs = nc.dram_tensor("d_mixs", [128, 4, NS], BF16, kind="ExternalOutput").ap()
        S.dma("sp", d_mix, mixT, reads=BmixA, sem="dbg")
        S.dma("sp", d_mixs, mixTs, reads=[BmixsA], sem="dbg")
    if stop_after == "A":
        S.finish()
        return nc, S


    esH = contextlib.ExitStack()
    mixH = sbt(esH, "mixH", (128, 4, NMAIN * 128), BF16)
    mixHs = sbt(esH, "mixHs", (128, 4, NS), BF16)
    lbb = sbt(esH, "lbb", (128, 2, 512)); BlbH = Buf()
    S.dma("sp", lbb, lbl.partition_broadcast(128), writes=[Bc2 := Buf()], sem="cH")
    S.seal([Bc2], "cH")
    S.op("dve", lambda e: e.tensor_tensor(out=lbb[:, 0], in0=lbb[:, 0], in1=lbb[:, 1], op=ALU.subtract),
         reads=[Bc2], writes=[BlbH])
    S.op("act", lambda e: e.activation(out=lbb[:, 1], in_=lbb[:, 0], func=AF.Sigmoid, scale=-1.0),
         reads=[BlbH], writes=[BlbH])
    S.op("act", lambda e: e.activation(out=lbb[:, 0], in_=lbb[:, 0], func=AF.Sigmoid), reads=[BlbH], writes=[BlbH])
    Wh = sbt(esH, "Wh", (128, 8, 2048), BF16); BWh = Buf()
    wload(Wh, w_in, 2048, 1536, [BWh], "wH")
    Wout = sbt(esH, "Wout", (128, 8, D), BF16)
    wload(Wout, w_out, D, 0, [BWh], "wH")
    Wcq = sbt(esH, "Wcq", (128, 8, 512), BF16)
    wload(Wcq, w_cq, 512, 0, [BWh], "wH")
    Wco = sbt(esH, "Wco", (128, 4, D), BF16)
    wload(Wco, w_co, D, 0, [BWh], "wH", nk=4)
    MKT = sbt(esH, "MKT", (128, 4, 256), BF16); BMKT = Buf()
    MV = sbt(esH, "MV", (128, 2, 512), BF16); BMV = Buf()
    xt = [sbt(esH, "xtH%d" % i, (128, D)) for i in range(3)]; Bxt = [Buf(), Buf(), Buf()]
    hT = [sbt(esH, "hTH%d" % i, (128, 8, 128), BF16) for i in range(2)]; BhT = [Buf(), Buf()]
    stg = [sbt(esH, "stg%d" % i, (128, 512)) for i in range(2)]; Bstg = [Buf(), Buf()]
    hrot = Rot([0, 1, 2])
    wrot.ids = [0, 1, 2]
    orot = Rot([6])
    brot = Rot([3, 4, 5])
    obrot = Rot([7])
    esM = contextlib.ExitStack()
    Wck = sbt(esM, "Wck", (128, 8, 512), BF16); Wcv = sbt(esM, "Wcv", (128, 8, 512), BF16)
    wload(Wck, w_ck, 512, 0, [BWh], "wH"); wload(Wcv, w_cv, 512, 0, [BWh], "wH")
    S.seal([BWh], "wH")
    nst = 0
    for mt in range(2):
        sl = mt % 2
        S.dma("sp", xt[sl], memp[mt * 128:(mt + 1) * 128, :], writes=[Bxt[sl]], sem="xH%d" % sl)
        norm_T(xt[sl], Bxt[sl], 128, 24, hT[sl], BhT[sl])
        bk, Bb = hrot.get(); bv = bk.rearrange("p (a b) -> p a b", a=4)
        proj_fm(bv, Bb, Wck, 0, 4, hT[sl], 128, [BWh, BhT[sl]])
        S.op("act", lambda e, bv=bv, mt=mt: e.activation(out=MKT[:, :, mt * 128:(mt + 1) * 128], in_=bv, func=AF.Copy),
             reads=[Bb], writes=[BMKT])
        for W_, dst in ((Wck, o_mk), (Wcv, o_mv)):
            bk, Bb = hrot.get()
            proj_tm(bk, Bb, W_, 0, 512, hT[sl], 128, [BWh, BhT[sl]])
            so = nst % 2; nst += 1
            S.op("act", lambda e, bk=bk, so=so: e.activation(out=stg[so], in_=bk, func=AF.Copy),
                 reads=[Bb], writes=[Bstg[so]])
            S.dma("pool", dst[mt * 128:(mt + 1) * 128, :], stg[so], reads=[Bstg[so]], sem="om%d" % so)
            if W_ is Wcv:
                S.op("dve", lambda e, bk=bk, mt=mt: e.tensor_copy(out=MV[:, mt, :], in_=bk), reads=[Bb], writes=[BMV])
    S.barrier()
    esM.close()

    wbes = [esH]

    def wb(name, shape, dt=F32):
        return sbt(wbes[0], name, shape, dt), Buf()
    cT, BcT = wb("cT", (128, 8, 128), BF16); cqT, BcqT = wb("cqT", (128, 4, 128), BF16)
    Pc = [wb("Pc%d" % i, (128, 4, 128), BF16) for i in range(2)]
    coT, BcoT = wb("coT", (128, 4, 128), BF16); rden, Brden = stg[0], Bstg[0]
    t_xb, Bt_xb = wb("t_xb", (128, 512), BF16)
    hsT = sbt(esH, "hsTH", (128, 8, NS), BF16); BhsT = Buf()
    esHw = contextlib.ExitStack()
    wbes[0] = esHw
    t_f, Bt_f = wb("t_f", (128, 512)); t_g, Bt_g = wb("t_g", (128, 512)); t_kk, Bt_kk = wb("t_kk", (128, 512))
    t_er, Bt_er = wb("t_er", (128, 512)); t_v, Bt_v = wb("t_v", (128, 512), BF16)
    khm = [wb("khm%d" % c, (128, 512), BF16) for c in range(4)]
    t_eb, Bt_eb = wb("t_eb", (128, 4, 128)); t_enb, Bt_enb = wb("t_enb", (128, 4, 128))
    t_q, Bt_q = wb("t_q", (128, 4, 128), BF16); t_sg, Bt_sg = wb("t_sg", (128, 4, 128), BF16)
    t_x, Bt_x = wb("t_x", (128, 512), BF16)
    t_qb, Bt_qb = wb("t_qb", (128, 512), BF16)
    t_gs, Bt_gs = wb("t_gs", (128, 4, 128), BF16)
    t_am, Bt_am = wb("t_am", (128, 4, 128), BF16); t_osq, Bt_osq = t_am, Bt_am
    t_rs, Bt_rs = t_er.rearrange("p (a b) -> p a b", a=4), Bt_er
    t_sgg, Bt_sgg = t_gs, Bt_gs
    NR = 5
    ring = [wb("Sring%d" % i, (128, 4, 128)) for i in range(NR)]
    ringb = [wb("Sringb%d" % i, (128, 4, 128), BF16) for i in range(NR)]
    Sos = [wb("Sos%d" % i, (128, 4, 128)) for i in range(2)]
    S.op("pool", lambda e: e.memset(ring[0][0], 0.0), writes=[ring[0][1]])
    S.op("pool", lambda e: e.memset(ringb[0][0], 0.0), writes=[ringb[0][1]])
    rcur = [0]

    def hgrn_tile(T, L, hTa, BhTa, masks, full, chain, mixdst, Bmixdst, s0list=None, sout=None, on_state=None,
                  need_shadow=False):
        BTm, M2m, CMm = masks
        yield
        bk, Bb = hrot.get()
        yield
        proj_tm(bk, Bb, Wh, 512, 512, hTa, T, [BWh, BhTa])
        yield
        S.op("act", lambda e: e.activation(out=t_f[:T], in_=bk[:T, :], func=AF.Sigmoid), reads=[Bb], writes=[Bt_f])
        yield
        bk2, Bb2 = hrot.get()
        yield
        proj_tm(bk2, Bb2, Wh, 1024, 512, hTa, T, [BWh, BhTa])
        yield
        S.op("act", lambda e: e.activation(out=t_v[:T], in_=bk2[:T, :], func=AF.Copy), reads=[Bb2], writes=[Bt_v])
        yield
        S.op("dve", lambda e: e.tensor_tensor(out=t_f[:T], in0=t_f[:T], in1=lbb[:T, 1], op=ALU.mult),
             reads=[BlbH], writes=[Bt_f])
        yield
        S.op("dve", lambda e: e.tensor_tensor(out=t_f[:T], in0=t_f[:T], in1=lbb[:T, 0], op=ALU.add),
             reads=[BlbH], writes=[Bt_f])
        yield
        S.op("act", lambda e: e.activation(out=t_g[:T], in_=t_f[:T], func=AF.Ln), reads=[Bt_f], writes=[Bt_g])
        yield
        S.op("pool", lambda e: e.tensor_scalar(out=t_kk[:T], in0=t_f[:T], scalar1=-1.0, scalar2=1.0, op0=ALU.mult,
                                               op1=ALU.add), reads=[Bt_f], writes=[Bt_kk])
        yield
        bk, Bb = hrot.get()
        yield
        S.group("pe", [mm(bk[:T, :], M2m[:T, :T], t_g[:T, :], True, True)], reads=[Bc, Bt_g], writes=[Bb])
        yield
        S.op("act", lambda e: e.activation(out=t_er[:T], in_=bk[:T, :], func=AF.Exp), reads=[Bb], writes=[Bt_er])
        yield
        if full:
            tv = lambda a: a.rearrange("p a b -> p (a b)")
            bkb, Bbb = hrot.get()
            S.group("pe", [mm(bkb[:T, :], BTm[:T, :T], t_g[:T, :], True, True)], reads=[Bc, Bt_g], writes=[Bbb])
            S.op("act", lambda e: e.activation(out=tv(t_enb)[:T], in_=bkb[:T, :], func=AF.Exp, scale=-1.0),
                 reads=[Bbb], writes=[Bt_enb])
            S.op("act", lambda e: e.activation(out=t_f[:T], in_=bkb[:T, :], func=AF.Exp), reads=[Bbb], writes=[Bt_f])
            S.op("dve", lambda e: e.tensor_tensor(out=t_x[:T], in0=t_kk[:T], in1=tv(t_enb)[:T], op=ALU.mult),
                 reads=[Bt_kk, Bt_enb], writes=[Bt_x])
        yield
        S.op("dve", lambda e: e.tensor_tensor(out=t_kk[:T], in0=t_kk[:T], in1=t_er[:T], op=ALU.mult),
             reads=[Bt_er], writes=[Bt_kk])
        yield
        for c in range(4):
            if c % 2 == 0:
                S.op("act", lambda e, c=c: e.activation(out=khm[c][0][:T], in_=t_kk[:T], func=AF.Copy,
                                                        scale=CMm[:T, c:c + 1]),
                     reads=[Bt_kk, Bc], writes=[khm[c][1]])
            else:
                S.op("pool", lambda e, c=c: e.tensor_scalar(out=khm[c][0][:T], in0=t_kk[:T], scalar1=CMm[:T, c:c + 1],
                                                            scalar2=1.0, op0=ALU.mult, op1=ALU.mult),
                     reads=[Bt_kk, Bc], writes=[khm[c][1]])
        yield
        bk, Bb = hrot.get(); bvb = bk.rearrange("p (a b) -> p a b", a=4)
        yield
        S.group("pe", [mm(bvb[:, hd, :T], t_g[:T, hd * 128:(hd + 1) * 128], BTm[:T, :T], True, True) for hd in range(4)],
                reads=[Bt_g, Bc], writes=[Bb])
        yield
        S.op("act", lambda e: e.activation(out=t_eb[:, :, :T], in_=bvb[:, :, :T], func=AF.Exp), reads=[Bb], writes=[Bt_eb])
        slots = []
        yield
        for c in range(4):
            if chain:
                cur = ring[rcur[0] % NR]; nxt = ring[(rcur[0] + 1) % NR]
                curb = ringb[rcur[0] % NR]; nxtb = ringb[(rcur[0] + 1) % NR]; rcur[0] += 1
            else:
                cur = s0list[c]; nxt = sout[c]
                curb = ringb[c]; nxtb = None
                S.op("pool", lambda e, cur=cur, curb=curb: e.tensor_copy(out=curb[0], in_=cur[0]),
                     reads=[cur[1]], writes=[curb[1]])
            slots.append(curb)
            bk, Bb = hrot.get(); bvk = bk.rearrange("p (a b) -> p a b", a=4)
            S.group("pe", [mm(bvk[:, hd, :], khm[c][0][:T, hd * 128:(hd + 1) * 128], t_v[:T, hd * 128:(hd + 1) * 128],
                              True, True) for hd in range(4)], reads=[khm[c][1], Bt_v], writes=[Bb])
            for hd in range(4):
                S.op("dve", lambda e, hd=hd, c=c, cur=cur, nxt=nxt, bvk=bvk: e.scalar_tensor_tensor(
                    out=nxt[0][:, hd, :], in0=cur[0][:, hd, :], scalar=t_eb[:, hd, c * L + L - 1:c * L + L],
                    in1=bvk[:, hd, :], op0=ALU.mult, op1=ALU.add), reads=[cur[1], Bt_eb, Bb], writes=[nxt[1]])
            if chain and (full or need_shadow):
                S.op("act", lambda e, nxt=nxt, nxtb=nxtb: e.activation(out=nxtb[0], in_=nxt[0], func=AF.Copy),
                     reads=[nxt[1]], writes=[nxtb[1]])
            if on_state is not None:
                on_state(c, nxt)
        yield
        if not full:
            return
        yield
        bk, Bb = hrot.get()
        yield
        proj_tm(bk, Bb, Wh, 0, 512, hTa, T, [BWh, BhTa])
        yield
        S.op("act", lambda e: e.activation(out=t_er[:T], in_=bk[:T, :], func=AF.Silu), reads=[Bb], writes=[Bt_er])
        yield
        S.op("dve", lambda e: e.scalar_tensor_tensor(out=t_qb[:T], in0=t_er[:T], scalar=float(128 ** -0.5),
                                                     in1=t_f[:T], op0=ALU.mult, op1=ALU.mult),
             reads=[Bt_f, Bt_er], writes=[Bt_qb])
        yield
        for src_, Bsrc_, dst_, Bdst_ in ((t_qb, Bt_qb, t_q, Bt_q), (t_x, Bt_x, t_sg, Bt_sg)):
            bk, Bb = hrot.get(); bvt = bk.bitcast(BF16).rearrange("p (a b) -> p a b", a=8)
            S.group("pe", [tr(bvt[:, hd, :T], src_[:T, hd * 128:(hd + 1) * 128], identb[:T, :T]) for hd in range(4)],
                    reads=[Bsrc_, Bones], writes=[Bb])
            S.op("act", lambda e, bvt=bvt, dst_=dst_: e.activation(out=dst_[:, :, :T], in_=bvt[:, 0:4, :T], func=AF.Copy),
                 reads=[Bb], writes=[Bdst_])
        yield
        bk, Bb = hrot.get(); bva = bk.rearrange("p (a b) -> p a b", a=4)
        yield
        S.group("pe", [mm(bva[:T, hd, :T], t_sg[:, hd, :T], t_q[:, hd, :T], True, True) for hd in range(4)],
                reads=[Bt_sg, Bt_q], writes=[Bb])
        yield
        S.op("dve", lambda e: e.tensor_tensor(out=t_am[:T, :, :T], in0=bva[:T, :, :T],
                                              in1=BTm[:T, :T].unsqueeze(1).to_broadcast([T, 4, T]), op=ALU.mult),
             reads=[Bb, Bc], writes=[Bt_am])
        yield
        obk, Bo = orot.get(); obv = obk.rearrange("p (a b) -> p a b", a=4)
        fns = []
        yield
        for hd in range(4):
            fns.append(mm(obv[:, hd, :T], t_v[:T, hd * 128:(hd + 1) * 128], t_am[:T, hd, :T], hd == 0, False,
                          skip_group_check=True))
        yield
        for hd in range(4):
            for c in range(4):
                fns.append(mm(obv[:, hd, c * L:(c + 1) * L], slots[c][0][:, hd, :], t_q[:, hd, c * L:(c + 1) * L],
                              False, (hd == 3 and c == 3), skip_group_check=True))
        yield
        S.group("pe", fns, reads=[Bt_v, Bt_am, Bt_q] + [sl_[1] for sl_ in slots], writes=[Bo])
        yield
        S.op("act", lambda e: e.activation(out=t_osq[:, :, :T], in_=obv[:, :, :T], func=AF.Square), reads=[Bo], writes=[Bt_osq])
        yield
        bk, Bb = hrot.get(); bvm = bk.rearrange("p (a b) -> p a b", a=4)
        yield
        S.group("pe", [mm(bvm[:, hd, :T], onesb, t_osq[:, hd, :T], True, True) for hd in range(4)],
                reads=[Bones, Bt_osq], writes=[Bb])
        yield
        S.op("act", lambda e: e.activation(out=t_rs[:, :, :T], in_=bvm[:, :, :T], func=AF.Ln, scale=1.0 / 128, bias=EPS),
             reads=[Bb], writes=[Bt_rs])
        yield
        S.op("act", lambda e: e.activation(out=t_rs[:, :, :T], in_=t_rs[:, :, :T], func=AF.Exp, scale=-0.5),
             reads=[Bt_rs], writes=[Bt_rs])
        yield
        S.op("dve", lambda e: e.tensor_tensor(out=t_rs[:, :, :T], in0=obv[:, :, :T], in1=t_rs[:, :, :T], op=ALU.mult),
             reads=[Bo, Bt_rs], writes=[Bt_rs])
        yield
        bk, Bb = hrot.get()
        yield
        proj_tm(bk, Bb, Wh, 1536, 512, hTa, T, [BWh, BhTa])
        yield
        S.op("act", lambda e: e.activation(out=t_x[:T], in_=bk[:T, :], func=AF.Silu), reads=[Bb], writes=[Bt_x])
        yield
        bk, Bb = hrot.get(); bvg = bk.bitcast(BF16).rearrange("p (a b) -> p a b", a=8)
        yield
        S.group("pe", [tr(bvg[:, hd, :T], t_x[:T, hd * 128:(hd + 1) * 128], identb[:T, :T]) for hd in range(4)],
                reads=[Bt_x, Bones], writes=[Bb])
        yield
        S.op("act", lambda e: e.activation(out=t_sgg[:, :, :T], in_=bvg[:, 0:4, :T], func=AF.Copy), reads=[Bb], writes=[Bt_sgg])
        yield
        for hd in range(4):
            S.op("dve", lambda e, hd=hd: e.scalar_tensor_tensor(
                out=mixdst[:, hd, :T], in0=t_rs[:, hd, :T], scalar=gcols[:, 36 + hd:37 + hd], in1=t_sgg[:, hd, :T],
                op0=ALU.mult, op1=ALU.mult), reads=[Bt_rs, Bt_sgg, Bc], writes=[Bmixdst])

    def wout_tile(T, xa, Bx, mA, BmA, mH, BmH):
        for hf in range(2):
            bk, Bb = brot.get()
            fns = []
            for k in range(8):
                src = mA[:, k, :T] if k < 4 else mH[:, k - 4, :T]
                fns.append(mm(bk[:T, :], src, Wout[:, k, hf * 512:(hf + 1) * 512], k == 0, k == 7))
            S.group("pe", fns, reads=[BmA, BmH, BWh], writes=[Bb])
            S.op("dve", lambda e, bk=bk, hf=hf: e.tensor_tensor(out=xa[:T, hf * 512:(hf + 1) * 512], in0=bk[:T, :],
                                                               in1=xa[:T, hf * 512:(hf + 1) * 512], op=ALU.add),
                 reads=[Bb], writes=[Bx])

    def wco_tile(T, xa, Bx):
        for hf in range(2):
            bk, Bb = brot.get()
            S.group("pe", [mm(bk[:T, :], coT[:, k, :T], Wco[:, k, hf * 512:(hf + 1) * 512], k == 0, k == 3)
                           for k in range(4)], reads=[BcoT, BWh], writes=[Bb])
            S.op("dve", lambda e, bk=bk, hf=hf: e.tensor_tensor(out=xa[:T, hf * 512:(hf + 1) * 512], in0=bk[:T, :],
                                                               in1=xa[:T, hf * 512:(hf + 1) * 512], op=ALU.add),
                 reads=[Bb], writes=[Bx])

    def cross_prompt(T):
        bk, Bb = brot.get()
        yield
        proj_tm(bk, Bb, Wcq, 0, 512, cT, T, [BWh, BcT])
        yield
        S.op("act", lambda e: e.activation(out=t_xb[:T], in_=bk[:T, :], func=AF.Copy), reads=[Bb], writes=[Bt_xb])
        yield
        bk, Bb = brot.get(); bvq = bk.bitcast(BF16).rearrange("p (a b) -> p a b", a=8)
        yield
        S.group("pe", [tr(bvq[:, hd, :T], t_xb[:T, hd * 128:(hd + 1) * 128], identb[:T, :T]) for hd in range(4)],
                reads=[Bt_xb, Bones], writes=[Bb])
        yield
        S.op("act", lambda e: e.activation(out=cqT[:, :, :T], in_=bvq[:, 0:4, :T], func=AF.Copy), reads=[Bb], writes=[BcqT])
        yield
        for mb in range(2):
            bk, Bb = brot.get(); bvs = bk.rearrange("p (a b) -> p a b", a=4)
            S.group("pe", [mm(bvs[:, hd, :T], MKT[:, hd, mb * 128:(mb + 1) * 128], cqT[:, hd, :T], True, True)
                           for hd in range(4)], reads=[BMKT, BcqT], writes=[Bb])
            S.op("act", lambda e, bvs=bvs, mb=mb: e.activation(out=Pc[mb][0][:, :, :T], in_=bvs[:, :, :T], func=AF.Exp,
                                                              scale=float(128 ** -0.5)),
                 reads=[Bb], writes=[Pc[mb][1]])
        yield
        obk, Bo = obrot.get(); obv = obk.rearrange("p (a b) -> p a b", a=4)
        fns = []
        for hd in range(4):
            for mb in range(2):
                fns.append(mm(obv[:, hd, :T], MV[:, mb, hd * 128:(hd + 1) * 128], Pc[mb][0][:, hd, :T], mb == 0, mb == 1))
        yield
        S.group("pe", fns, reads=[BMV, Pc[0][1], Pc[1][1]], writes=[Bo])
        yield
        bk, Bb = brot.get(); bvd = bk.rearrange("p (a b) -> p a b", a=4)
        fns = []
        for hd in range(4):
            for mb in range(2):
                fns.append(mm(bvd[:, hd, :T], onesb, Pc[mb][0][:, hd, :T], mb == 0, mb == 1))
        yield
        S.group("pe", fns, reads=[Bones, Pc[0][1], Pc[1][1]], writes=[Bb])
        yield
        rdv = rden.rearrange("p (a b) -> p a b", a=4)
        yield
        S.op("act", lambda e: e.activation(out=rdv[:, :, :T], in_=bvd[:, :, :T], func=AF.Ln), reads=[Bb], writes=[Brden])
        S.op("act", lambda e: e.activation(out=rdv[:, :, :T], in_=rdv[:, :, :T], func=AF.Exp, scale=-1.0),
             reads=[Brden], writes=[Brden])
        yield
        S.op("dve", lambda e: e.tensor_tensor(out=coT[:, :, :T], in0=obv[:, :, :T], in1=rdv[:, :, :T], op=ALU.mult),
             reads=[Bo, Brden], writes=[BcoT])

    def run_threads(gens):
        gens = [g for g in gens if g is not None]
        while gens:
            for g in list(gens):
                try:
                    next(g)
                except StopIteration:
                    gens.remove(g)

    def thrA(ti):
        sl = ti % 3
        main = ti >= T0
        if ti + 1 < NT:
            S.dma("sp", xt[(ti + 1) % 3], xc[(ti + 1) * 128:(ti + 2) * 128, :], writes=[Bxt[(ti + 1) % 3]],
                  sem="xH%d" % ((ti + 1) % 3))
        norm_T(xt[sl], Bxt[sl], 128, 0, hT[ti % 2], BhT[ti % 2], rot=hrot)
        yield
        mt = ti - T0
        yield from hgrn_tile(128, 32, hT[ti % 2], BhT[ti % 2], (btp, m2p, cmp_), main, True,
                             mixH[:, :, mt * 128:(mt + 1) * 128] if main else None, BmixH[mt] if main else None,
                             need_shadow=(ti == T0 - 1))

    def thrB(ti):
        sl = ti % 3
        mt = ti - T0
        wout_tile(128, xt[sl], Bxt[sl], mixT[:, :, mt * 128:(mt + 1) * 128], BmixA[mt],
                  mixH[:, :, mt * 128:(mt + 1) * 128], BmixH[mt])
        yield
        norm_T(xt[sl], Bxt[sl], 128, 8, cT, BcT, rot=brot)
        yield
        yield from cross_prompt(128)
        wco_tile(128, xt[sl], Bxt[sl])
        yield
        S.dma("pool", x2d[mt * 128:(mt + 1) * 128, :], xt[sl], reads=[Bxt[sl]], writes=[Bx2d[mt]], sem="x2w%d" % sl)

    S.dma("sp", xt[0], xc[0:128, :], writes=[Bxt[0]], sem="xH0")
    pend = None
    for ti in range(NT):
        run_threads([thrA(ti), pend])
        pend = thrB(ti) if ti >= T0 else None
    run_threads([pend])
    fin = ring[rcur[0] % NR]
    S.dma("pool", o_hs.rearrange("h k v -> k h v"), fin[0], reads=[fin[1]], sem="ohs")

    xst = xt[0][:NS]; Bxst = Bxt[0]
    S0all = [ring[(rcur[0] + 1 + i) % NR] for i in range(NSQ)]
    S.dma("sp", xst, xs, writes=[Bxst], sem="xsH")
    for i in range(NSQ):
        S.dma("sp", S0all[i][0], shg[i].rearrange("h k v -> k h v"), writes=[S0all[i][1]], sem="s0_%d" % i)
    norm_T(xst, Bxst, NS, 0, hsT, BhsT)

    def emit_state(c, nxt):
        S.dma("pool", o_hss[c].rearrange("h k v -> k h v"), nxt[0], reads=[nxt[1]], sem="ohss%d" % (c % 2))

    for _ in hgrn_tile(NS, 4, hsT, BhsT, (bts, m2s, cms), True, False, mixHs, BmixsH, s0list=S0all,
                       sout=[Sos[c % 2] for c in range(NSQ)], on_state=emit_state):
        pass
    wout_tile(NS, xst, Bxst, mixTs, BmixsA, mixHs, BmixsH)
    norm_T(xst, Bxst, NS, 8, cT, BcT)
    if debug:
        d_og = nc.dram_tensor("d_og", [128, 4, NMAIN * 128], BF16, kind="ExternalOutput").ap()
        d_ogs = nc.dram_tensor("d_ogs", [128, 4, NS], BF16, kind="ExternalOutput").ap()
        S.dma("sp", d_og, mixH, reads=BmixH, sem="dbg")
        S.dma("sp", d_ogs, mixHs, reads=[BmixsH], sem="dbg")
    S.barrier()
    esHw.close()
    esHs = contextlib.ExitStack()
    selb = sbt(esHs, "selbH", (16, 16, 128)); Bsel = Buf()
    S.dma("sp", selb, c_selb, writes=[Bsel], sem="cHs"); S.seal([Bsel], "cHs")
    cqs = sbt(esHs, "cqs", (NS, 512)); Bcqs = Buf()
    MKs = [sbt(esHs, "MKs%d" % i, (128, 2, 512)) for i in range(2)]; BMKs = [Buf(), Buf()]
    MVs = [sbt(esHs, "MVs%d" % i, (128, 2, 4, 129)) for i in range(2)]; BMVs = [Buf(), Buf()]
    for i in range(2):
        S.op("pool", lambda e, i=i: e.memset(MVs[i], 1.0), writes=[BMVs[i]])
    prodc = sbt(esHs, "prodc", (128, 2, 512)); Bprodc = Buf()
    SCc = sbt(esHs, "SCc", (128, 2, 4, 4)); BSCc = Buf()
    PCz = [sbt(esHs, "PCz%d" % i, (128, 2, NS, 4)) for i in range(NSQ)]; BPCz = [Buf() for _ in range(NSQ)]
    for i in range(NSQ):
        S.op("pool", lambda e, i=i: e.memset(PCz[i], 0.0), writes=[BPCz[i]])
    bk, Bb = hrot.get()
    proj_tm(bk, Bb, Wcq, 0, 512, cT, NS, [BWh, BcT])
    S.op("act", lambda e, bk=bk: e.activation(out=cqs, in_=bk[:NS, :], func=AF.Copy), reads=[Bb], writes=[Bcqs])
    obc = [banks[6][:NS, 0:258].rearrange("p (h d) -> p h d", h=2), banks[7][:NS, 0:258].rearrange("p (h d) -> p h d", h=2)]
    Bobc = [Bbank[6], Bbank[7]]
    firstc = [True, True]
    for s_ in range(NSQ):
        sl = s_ % 2
        S.dma("sp", MKs[sl], cmk[s_].rearrange("(b p) c -> p b c", p=128), writes=[BMKs[sl]], sem="mk%d" % sl)
        for blk in range(2):
            S.dma("sp", MVs[sl][:, blk, :, 0:128],
                  cmv[s_, blk * 128:(blk + 1) * 128, :].rearrange("p (h d) -> p h d", h=4), writes=[BMVs[sl]],
                  sem="mv%d" % sl)
        for i in range(4):
            tq = 4 * s_ + i
            bk, Bb = hrot.get()
            S.group("pe", [mm(bk[:, :], selb[:, tq, :], cqs[:, 0:512], True, True)], reads=[Bsel, Bcqs], writes=[Bb])
            S.op("dve", lambda e, bk=bk, sl=sl: e.tensor_tensor(
                out=prodc, in0=MKs[sl], in1=bk.unsqueeze(1).to_broadcast([128, 2, 512]), op=ALU.mult),
                reads=[BMKs[sl], Bb], writes=[Bprodc])
            S.op("dve", lambda e, i=i: e.tensor_reduce(
                out=SCc[:, :, i, :], in_=prodc.rearrange("p b (h d) -> p b h d", h=4), axis=AX.X, op=ALU.add),
                reads=[Bprodc], writes=[BSCc])
        S.op("act", lambda e, s_=s_: e.activation(out=PCz[s_][:, :, 4 * s_:4 * s_ + 4, :], in_=SCc, func=AF.Exp,
                                                  scale=float(128 ** -0.5)), reads=[BSCc], writes=[BPCz[s_]])
        for blk in range(2):
            for g in range(2):
                fns = []
                for hh in range(2):
                    fns.append(mm(obc[g][:, hh, :], PCz[s_][:, blk, :, 2 * g + hh], MVs[sl][:, blk, 2 * g + hh, :],
                                  firstc[g] and hh == 0, False, skip_group_check=True))
                firstc[g] = False
                S.group("pe", fns, reads=[BPCz[s_], BMVs[sl]], writes=[Bobc[g]])
    attn_finish(NS, obc, Bobc, 4, 128, None, None, None)
    tm_to_fm(NS, on_sb, Bon, 4, None, coT, BcoT)
    wco_tile(NS, xst, Bxst)
    S.dma("sp", x2d[NMAIN * 128:NMAIN * 128 + NS, :], xst, reads=[Bxst], writes=[Bx2d[NMAIN]], sem="x2ws")
    S.barrier()
    esHs.close()
    esH.close()

    if False:
        d_og = nc.dram_tensor("d_og", [128, 4, NMAIN * 128], BF16, kind="ExternalOutput").ap()
        d_ogs = nc.dram_tensor("d_ogs", [128, 4, NS], BF16, kind="ExternalOutput").ap()
        S.dma("sp", d_og, mixH, reads=BmixH, sem="dbg")
        S.dma("sp", d_ogs, mixHs, reads=[BmixsH], sem="dbg")
    if stop_after == "H0":
        S.finish()
        return nc, S

    esF = contextlib.ExitStack()
    hrot = Rot([0, 1, 2, 3, 4, 5])
    wrot.ids = [0, 1, 2, 3, 4, 5]
    orot = Rot([6, 7])
    gfin = sbt(esF, "gfin", (128, D)); BcF = Buf()
    S.dma("sp", gfin, norm_final.rearrange("(o d) -> o d", o=1).partition_broadcast(128)[:, 0, :], writes=[BcF], sem="cF")
    S.seal([BcF], "cF")
    Wd = sbt(esF, "Wd", (128, NF, D), BF16); BWd = Buf()
    wdv = w_down.rearrange("(f p) c -> p f c", p=128)
    for f in range(NF):
        S.dma("pool", Wd[:, f, :], wdv[:, f, :], writes=[BWd], sem="wd")
    S.seal([BWd], "wd")
    uT = sbt(esF, "uT", (128, 8, 512), BF16); BuT = [Buf() for _ in range(4)]
    usT = sbt(esF, "usT", (128, 8, NS), BF16); BusT = Buf()
    uhT = sbt(esF, "uhT", (128, 8, 128), BF16); BuhT = Buf()
    xt2 = [sbt(esF, "xt2_%d" % i, (128, D)) for i in range(4)]; Bxt2 = [Buf() for _ in range(4)]
    xs2 = sbt(esF, "xs2", (NS, D)); Bxs2 = Buf()
    aT = sbt(esF, "aT", (128, NF, 512), BF16); BaT = [Buf() for _ in range(NF)]
    aTs = sbt(esF, "aTs", (128, NF, NS), BF16); BaTs = Buf()
    NWS = 3
    Wg = [sbt(esF, "Wg%d" % i, (128, 8, 256), BF16) for i in range(NWS)]
    Wu = [sbt(esF, "Wu%d" % i, (128, 8, 256), BF16) for i in range(NWS)]
    BWs = [Buf() for _ in range(NWS)]
    ubs = [sbt(esF, "ubs%d" % i, (128, 512), BF16) for i in range(2)]; Bubs = [Buf(), Buf()]
    sil = [sbt(esF, "sil%d" % i, (128, 512), BF16) for i in range(2)]; Bsil = [Buf(), Buf()]
    acc = [sbt(esF, "acc%d" % i, (128, 512)) for i in range(2)]; Bacc = [Buf(), Buf()]
    halo = sbt(esF, "halo", (128, NF, 2)); Bhalo = [Buf() for _ in range(NF)]
    bufT = sbt(esF, "bufT", (128, NF, 8)); BbufT = Buf()
    sfct = sbt(esF, "sfct", (8, DFF)); Bsfct = Buf()
    gexs = sbt(esF, "gexs", (128, 4, 6)); Bgexs = Buf()
    accs = sbt(esF, "accs", (128, 4, 4)); Baccs = Buf()
    fst = [sbt(esF, "fst%d" % i, (NS, 128)) for i in range(2)]; Bfst = [Buf(), Buf()]
    fsp = [sbt(esF, "fsp%d" % i, (2, 512)) for i in range(2)]; Bfsp = [Buf(), Buf()]
    ucat = sbt(esF, "ucat", (128, 8, NS + 2), BF16); Bucat = Buf()
    gsc = sbt(esF, "gsc", (128, NS)); Bgsc = Buf()
    wgv = w_gate.rearrange("(k p) c -> p k c", p=128)
    wuv = w_up.rearrange("(k p) c -> p k c", p=128)
    NG = NF // 2
    NBLK = 4

    def wstream(gi):
        fg = gi % NG
        sl = gi % NWS
        S.dma("pool", Wg[sl], wgv[:, :, fg * 256:(fg + 1) * 256], writes=[BWs[sl]], sem="ws%d" % sl)
        S.dma("pool", Wu[sl], wuv[:, :, fg * 256:(fg + 1) * 256], writes=[BWs[sl]], sem="ws%d" % sl)

    wstream(0); wstream(1)
    S.dma("sp", xt2[0], x2d[0:128, :], reads=[Bx2d[0]], writes=[Bxt2[0]], sem="xF0")
    norm_T(xt2[0], Bxt2[0], 128, 16, uhT, BuhT)
    S.dma("sp", xs2, x2d[NMAIN * 128:NMAIN * 128 + NS, :], reads=[Bx2d[NMAIN]], writes=[Bxs2], sem="xFs")
    norm_T(xs2, Bxs2, NS, 16, usT, BusT)
    S.op("pool", lambda e: e.tensor_copy(out=ucat[:, :, 0:NS], in_=usT), reads=[BusT], writes=[Bucat])
    S.op("pool", lambda e: e.tensor_copy(out=ucat[:, :, NS:NS + 2], in_=uhT[:, :, 126:128]), reads=[BuhT], writes=[Bucat])
    S.dma("sp", sfct, sfc, writes=[Bsfct], sem="sfct")
    for f0 in range(0, NF, 4):
        nf_ = min(4, NF - f0)
        bk, Bb = hrot.get(); bv = bk.rearrange("p (a b) -> p a b", a=4)
        S.group("pe", [tr(bv[:, c, :8], sfct[:8, (f0 + c) * 128:(f0 + c + 1) * 128], ident[:8, :8]) for c in range(nf_)],
                reads=[Bsfct, Bc], writes=[Bb])
        S.op("act", lambda e, bv=bv, f0=f0, nf_=nf_: e.activation(out=bufT[:, f0:f0 + nf_, :], in_=bv[:, 0:nf_, :8],
                                                               func=AF.Copy), reads=[Bb], writes=[BbufT])
    nfs = [0]
    gi = 0
    for bi in range(NBLK):
        for r in range(4):
            mt = 1 + 4 * bi + r
            S.dma("sp", xt2[r], x2d[mt * 128:(mt + 1) * 128, :], reads=[Bx2d[mt]], writes=[Bxt2[r]], sem="xF%d" % r)
            norm_T(xt2[r], Bxt2[r], 128, 16, uT[:, :, r * 128:(r + 1) * 128], BuT[r])
        for fg in range(NG):
            if gi + 2 < NBLK * NG:
                wstream(gi + 2)
            sl = gi % NWS
            gi += 1
            for f2 in range(2):
                f = 2 * fg + f2
                e_ = f % 2
                wg = lambda k: Wg[sl][:, k, f2 * 128:(f2 + 1) * 128]
                wu = lambda k: Wu[sl][:, k, f2 * 128:(f2 + 1) * 128]
                gb, Bg = hrot.get()
                S.group("pe", [mm(gb[:, :], wg(k), uT[:, k, :], k == 0, k == 7) for k in range(8)],
                        reads=[BWs[sl]] + BuT, writes=[Bg])
                ub, Bu = hrot.get()
                S.group("pe", [mm(ub[:, :], wu(k), uT[:, k, :], k == 0, k == 7) for k in range(8)],
                        reads=[BWs[sl]] + BuT, writes=[Bu])
                if bi == 0:
                    xb, Bx = hrot.get()
                    fns = [mm(xb[:, 0:NS + 2], wg(k), ucat[:, k, :], k == 0, k == 7) for k in range(8)]
                    fns += [mm(xb[:, 32:32 + NS], wu(k), usT[:, k, :], k == 0, k == 7) for k in range(8)]
                    S.group("pe", fns, reads=[BWs[sl], BusT, Bucat], writes=[Bx])
                    S.op("dve", lambda e, xb=xb, f=f: e.tensor_scalar(out=halo[:, f, :], in0=xb[:, NS:NS + 2],
                                                                      scalar1=hfl[:, 0:1], scalar2=None, op0=ALU.mult),
                         reads=[Bx, Bc], writes=[Bhalo[f]])
                    S.op("dve", lambda e, xb=xb: e.tensor_copy(out=gsc, in_=xb[:, 0:NS]), reads=[Bx], writes=[Bgsc])
                    xb2, Bx2 = hrot.get()
                    S.group("pe", [tr(xb2[:NS, 0:128], gsc[:, :NS], ident)], reads=[Bgsc, Bc], writes=[Bx2])
                    so = nfs[0] % 2; nfs[0] += 1
                    S.op("dve", lambda e, xb2=xb2, so=so: e.tensor_copy(out=fst[so], in_=xb2[:NS, 0:128]),
                         reads=[Bx2], writes=[Bfst[so]])
                    S.dma("sp", o_fcs[:, f * 128:(f + 1) * 128], fst[so], reads=[Bfst[so]], sem="ofs%d" % so)
                    S.op("pool", lambda e, f=f: e.tensor_copy(out=gexs[:, :, 0:2],
                                                              in_=bufT[:, f, :].rearrange("p (s j) -> p s j", s=4)),
                         reads=[BbufT], writes=[Bgexs])
                    S.op("dve", lambda e, xb=xb: e.tensor_copy(out=gexs[:, :, 2:6],
                                                               in_=xb[:, 0:NS].rearrange("p (s i) -> p s i", s=4)),
                         reads=[Bx], writes=[Bgexs])
                    S.op("dve", lambda e, f=f: e.tensor_scalar(out=accs, in0=gexs[:, :, 2:6], scalar1=cwc[:, f, 2:3],
                                                               scalar2=cbc[:, f:f + 1], op0=ALU.mult, op1=ALU.add),
                         reads=[Bgexs, Bc], writes=[Baccs])
                    for j_ in (1, 0):
                        S.op("dve", lambda e, f=f, j_=j_: e.scalar_tensor_tensor(
                            out=accs, in0=gexs[:, :, j_:j_ + 4], scalar=cwc[:, f, j_:j_ + 1], in1=accs,
                            op0=ALU.mult, op1=ALU.add), reads=[Bgexs, Bc], writes=[Baccs])
                    S.op("act", lambda e: e.activation(out=accs, in_=accs, func=AF.Silu), reads=[Baccs], writes=[Baccs])
                    S.op("dve", lambda e, xb=xb, f=f: e.tensor_tensor(
                        out=aTs[:, f, :].rearrange("p (s i) -> p s i", s=4), in0=accs,
                        in1=xb[:, 32:32 + NS].rearrange("p (s i) -> p s i", s=4), op=ALU.mult),
                        reads=[Baccs, Bx], writes=[BaTs])
                S.op("act", lambda e, e_=e_, f=f, gb=gb: e.activation(out=acc[e_], in_=gb, func=AF.Identity,
                                                                    scale=cwc[:, f, 2:3], bias=cbc[:, f:f + 1]),
                     reads=[Bg, Bc], writes=[Bacc[e_]])
                S.op("act", lambda e, e_=e_, ub=ub: e.activation(out=ubs[e_], in_=ub, func=AF.Copy),
                     reads=[Bu], writes=[Bubs[e_]])
                S.op("dve", lambda e, e_=e_, f=f, gb=gb: e.scalar_tensor_tensor(
                    out=acc[e_][:, 1:512], in0=gb[:, 0:511], scalar=cwc[:, f, 1:2], in1=acc[e_][:, 1:512],
                    op0=ALU.mult, op1=ALU.add), reads=[Bg, Bc], writes=[Bacc[e_]])
                S.op("dve", lambda e, e_=e_, f=f, gb=gb: e.scalar_tensor_tensor(
                    out=acc[e_][:, 2:512], in0=gb[:, 0:510], scalar=cwc[:, f, 0:1], in1=acc[e_][:, 2:512],
                    op0=ALU.mult, op1=ALU.add), reads=[Bg, Bc], writes=[Bacc[e_]])
                S.op("dve", lambda e, e_=e_, f=f: e.scalar_tensor_tensor(
                    out=acc[e_][:, 0:2], in0=halo[:, f, 0:2], scalar=cwc[:, f, 0:1], in1=acc[e_][:, 0:2],
                    op0=ALU.mult, op1=ALU.add), reads=[Bhalo[f], Bc], writes=[Bacc[e_]])
                S.op("dve", lambda e, e_=e_, f=f: e.scalar_tensor_tensor(
                    out=acc[e_][:, 0:1], in0=halo[:, f, 1:2], scalar=cwc[:, f, 1:2], in1=acc[e_][:, 0:1],
                    op0=ALU.mult, op1=ALU.add), reads=[Bhalo[f], Bc], writes=[Bacc[e_]])
                S.op("dve", lambda e, f=f, gb=gb: e.tensor_copy(out=halo[:, f, :], in_=gb[:, 510:512]),
                     reads=[Bg], writes=[Bhalo[f]])
                S.op("act", lambda e, e_=e_: e.activation(out=sil[e_], in_=acc[e_], func=AF.Silu),
                     reads=[Bacc[e_]], writes=[Bsil[e_]])
                S.op("pool", lambda e, e_=e_, f=f: e.tensor_tensor(out=aT[:, f, :], in0=sil[e_], in1=ubs[e_],
                                                                   op=ALU.mult),
                     reads=[Bsil[e_], Bubs[e_]], writes=[BaT[f]])
        if bi == NBLK - 1:
            for f0 in range(0, NF, 4):
                nf_ = min(4, NF - f0)
                xb, Bx = hrot.get()
                S.group("pe", [tr(xb[:2, c * 128:(c + 1) * 128], halo[:, f0 + c, :], ident) for c in range(nf_)],
                        reads=[Bhalo[f0 + c] for c in range(nf_)] + [Bc], writes=[Bx])
                so = (f0 // 4) % 2
                S.op("dve", lambda e, xb=xb, so=so, nf_=nf_: e.tensor_copy(out=fsp[so][:, 0:nf_ * 128], in_=xb[:2, 0:nf_ * 128]),
                     reads=[Bx], writes=[Bfsp[so]])
                S.dma("sp", o_fc[:, f0 * 128:(f0 + nf_) * 128], fsp[so][:, 0:nf_ * 128], reads=[Bfsp[so]], sem="ofp%d" % so)
        def down_tile(T, aTa, BaTa, xa, Bx, dst, dsem):
            for hf in range(2):
                ob_, Bo_ = orot.get()
                S.group("pe", [mm(ob_[:T, :], aTa(f), Wd[:, f, hf * 512:(hf + 1) * 512], f == 0, f == NF - 1)
                               for f in range(NF)], reads=[BWd] + BaTa, writes=[Bo_])
                S.op("dve", lambda e, ob_=ob_, hf=hf: e.tensor_tensor(out=xa[:T, hf * 512:(hf + 1) * 512], in0=ob_[:T, :],
                                                                     in1=xa[:T, hf * 512:(hf + 1) * 512], op=ALU.add),
                     reads=[Bo_], writes=[Bx])
            S.op("act", lambda e: e.activation(out=junk[:T], in_=xa[:T], func=AF.Square, accum_out=stat[:T, 0:1]),
                 reads=[Bx], writes=[Bjunk, Bstat])
            rstd_from_ss(T, slice(0, 1), slice(1, 2), float(D))
            S.op("dve", lambda e: e.scalar_tensor_tensor(out=xa[:T], in0=xa[:T], scalar=stat[:T, 1:2], in1=gfin[:T],
                                                         op0=ALU.mult, op1=ALU.mult), reads=[Bstat, BcF], writes=[Bx])
            S.dma("sp", dst, xa[:T], reads=[Bx], sem=dsem)

        for r in range(4):
            down_tile(128, lambda f, r=r: aT[:, f, r * 128:(r + 1) * 128], BaT, xt2[r], Bxt2[r],
                      o_y[(4 * bi + r) * 128:(4 * bi + r + 1) * 128, :], "oy%d" % r)
        if bi == 0:
            down_tile(NS, lambda f: aTs[:, f, :], [BaTs], xs2, Bxs2, o_ys, "oys")
    S.barrier()
    esF.close()
    S.finish()
    return nc, S


_NC_CACHE = {}


def _prep(inp):
    cs = _consts()
    f = lambda a: np.ascontiguousarray(np.asarray(a, dtype=np.float32))
    xp = f(inp["x_prompt"]); xsm = f(inp["x_sample"])
    shared = {
        "hg_lb_logits": f(inp["hg_lb_logits"]), "norm_mix": f(inp["norm_mix"])[0], "w_in": f(inp["w_in"])[0],
        "att_out_norm": f(inp["att_out_norm"])[0], "hg_out_norm": f(inp["hg_out_norm"])[0],
        "w_out": f(inp["w_out"])[0], "norm_cross": f(inp["norm_cross"])[0], "norm_mem": f(inp["norm_mem"])[0],
        "w_cq": f(inp["w_cq"])[0], "w_ck": f(inp["w_ck"])[0], "w_cv": f(inp["w_cv"])[0], "w_co": f(inp["w_co"])[0],
        "norm_ffn": f(inp["norm_ffn"])[0], "w_gate": f(inp["w_gate"])[0], "w_up": f(inp["w_up"])[0],
        "conv_w": f(inp["conv_w"])[0], "conv_b": f(inp["conv_b"])[0], "w_down": f(inp["w_down"])[0],
        "norm_final": f(inp["norm_final"]),
    }
    shared.update(cs)
    maps = []
    for c in range(8):
        b, half = c // 2, c % 2
        m = dict(shared)
        xcore = np.zeros((NT * 128, D), np.float32)
        vf = np.zeros((NT * 128,), np.float32)
        if half == 1:
            xcore[:] = xp[b]
            vf[:] = 1.0
        else:
            xcore[2048:] = xp[b, :2048]
            vf[2048:] = 1.0
        m["xc"] = xcore
        m["vflag"] = np.ascontiguousarray(vf.reshape(NT, 128).T)
        m["hflag"] = np.full((128, 1), float(half), np.float32)
        sq = slice(4 * c, 4 * c + 4)
        m["xs"] = np.ascontiguousarray(xsm[sq].reshape(NS, D))
        m["cwk"] = np.ascontiguousarray(f(inp["cache_win_k"])[0, sq].reshape(NSQ, 2048, 512))
        m["cwv"] = np.ascontiguousarray(f(inp["cache_win_v"])[0, sq].reshape(NSQ, 2048, 512))
        m["shg"] = np.ascontiguousarray(f(inp["state_hgrn"])[0, sq])
        m["sfc"] = np.ascontiguousarray(f(inp["state_ffn_conv"])[0, sq].reshape(NSQ * 2, DFF))
        m["cmk"] = np.ascontiguousarray(f(inp["cache_mem_k"])[0, sq].reshape(NSQ, 256, 512))
        m["cmv"] = np.ascontiguousarray(f(inp["cache_mem_v"])[0, sq].reshape(NSQ, 256, 512))
        m["memp"] = np.ascontiguousarray(f(inp["mem_prompt"])[b])
        maps.append(m)
    return maps


def _assemble(res):
    R = res.results
    yp = np.zeros((4, 4096, D), np.float32)
    ys = np.zeros((32, 4, D), np.float32)
    wk = np.zeros((1, 4, 2048, 8, 64), np.float32); wv = np.zeros_like(wk)
    hs = np.zeros((1, 4, 4, 128, 128), np.float32)
    fc = np.zeros((1, 4, 2, DFF), np.float32)
    mk = np.zeros((1, 4, 256, 4, 128), np.float32); mv = np.zeros_like(mk)
    wks = np.zeros((1, 32, 4, 8, 64), np.float32); wvs = np.zeros_like(wks)
    hss = np.zeros((1, 32, 4, 128, 128), np.float32)
    fcs = np.zeros((1, 32, 2, DFF), np.float32)
    for c in range(8):
        b, half = c // 2, c % 2
        r = R[c]
        yp[b, half * 2048:(half + 1) * 2048] = r["o_y"]
        sq = slice(4 * c, 4 * c + 4)
        ys[sq] = r["o_ys"].reshape(4, 4, D)
        wks[0, sq] = r["o_wks"].reshape(4, 4, 8, 64)
        wvs[0, sq] = r["o_wvs"].reshape(4, 4, 8, 64)
        hss[0, sq] = r["o_hss"]
        fcs[0, sq] = r["o_fcs"].reshape(4, 4, DFF)[:, 2:4]
        if half == 1:
            wk[0, b] = r["o_wk"].reshape(2048, 8, 64)
            wv[0, b] = r["o_wv"].reshape(2048, 8, 64)
            hs[0, b] = r["o_hs"]
            fc[0, b] = r["o_fc"]
            mk[0, b] = r["o_mk"].reshape(256, 4, 128)
            mv[0, b] = r["o_mv"].reshape(256, 4, 128)
    return (yp, ys, wk, wv, hs, fc, mk, mv, wks, wvs, hss, fcs)


def kernel(**inputs):
    if "nc" not in _NC_CACHE:
        _NC_CACHE["nc"] = build_nc()[0]
    nc = _NC_CACHE["nc"]
    maps = _prep(inputs)
    res = run_bass_kernel_spmd(nc, maps, core_ids=list(range(8)))
    return _assemble(res)
```

```python
import contextlib
import os
import numpy as np
import concourse.bass as bass
import concourse.mybir as mybir
from concourse.bass_utils import run_bass_kernel_spmd

F32 = mybir.dt.float32
BF16 = mybir.dt.bfloat16
AF = mybir.ActivationFunctionType
ALU = mybir.AluOpType
AX = mybir.AxisListType

D = 1024
NT = 32
T0 = 15
NMAIN = NT - T0
NOUT = 16
DFF = 2816
NF = 22
EPS = 1e-6
SEM_LIMIT = 30000
NSQ = 4
NS = 16


class Buf:
    __slots__ = ("name", "w", "r", "x")

    def __init__(self, name="", x=False):
        self.name = name
        self.w = {}
        self.r = {}
        self.x = x


class Sched:
    def __init__(self, nc):
        self.nc = nc
        self.eng = {"pe": nc.tensor, "act": nc.scalar, "dve": nc.vector,
                    "pool": nc.gpsimd, "sp": nc.sync}
        self.sems = {}
        self.cnt = {}
        self.seen = {e: {} for e in self.eng}
        self.final = {}
        self.nsem = 0
        self.ninst = 0
        self.nwait = 0

    def _stream(self, name):
        ep, c = self.cnt.get(name, (0, 0))
        if c >= SEM_LIMIT:
            ep, c = ep + 1, 0
        key = "%s#%d" % (name, ep)
        if key not in self.sems:
            self.sems[key] = self.nc.alloc_semaphore("s%d_%s" % (self.nsem, name))
            self.nsem += 1
        return key, ep, c

    def _deps(self, reads, writes):
        deps = {}
        for b in reads:
            for k, v in b.w.items():
                if deps.get(k, 0) < v:
                    deps[k] = v
            if b.x:
                for k, v in b.r.items():
                    if deps.get(k, 0) < v:
                        deps[k] = v
        for b in writes:
            for d in (b.w, b.r):
                for k, v in d.items():
                    if deps.get(k, 0) < v:
                        deps[k] = v
        return deps

    def _wait(self, ename, deps):
        e = self.eng[ename]
        seen = self.seen[ename]
        for k, v in deps.items():
            if ename == "pe" and k.startswith("pe#"):
                continue
            if seen.get(k, 0) < v:
                e.wait_ge(self.sems[k], v)
                seen[k] = v
                self.nwait += 1

    def _commit(self, key, val, reads, writes):
        for b in reads:
            if b.r.get(key, 0) < val:
                b.r[key] = val
        for b in writes:
            b.w = {key: val}
            b.r = {}

    def op(self, ename, fn, reads=(), writes=()):
        self.group(ename, [fn], reads, writes)

    def group(self, ename, fns, reads=(), writes=()):
        self._wait(ename, self._deps(reads, writes))
        key, ep, c = self._stream(ename)
        ins = None
        for fn in fns:
            ins = fn(self.eng[ename])
            self.ninst += 1
        ins.then_inc(self.sems[key], 1)
        c += 1
        self.cnt[ename] = (ep, c)
        self._commit(key, c, reads, writes)

    def dma(self, qname, out, in_, reads=(), writes=(), sem="dma", **kw):
        self._wait(qname, self._deps(reads, writes))
        sname = "d_" + sem
        key, ep, c = self._stream(sname)
        ins = self.eng[qname].dma_start(out=out, in_=in_, **kw)
        ins.then_inc(self.sems[key], 16)
        c += 16
        self.cnt[sname] = (ep, c)
        self.final[key] = c
        self._commit(key, c, reads, writes)
        self.ninst += 1

    def seal(self, bufs, sem):
        sname = "d_" + sem
        key, ep, c = self._stream(sname)
        for b in bufs:
            b.w = {key: c}

    def barrier(self):
        allv = dict(self.final)
        for name, (ep, c) in self.cnt.items():
            if c > 0:
                allv["%s#%d" % (name, ep)] = c
        for e in self.eng:
            self._wait(e, allv)

    def finish(self):
        self._wait("sp", dict(self.final))


def mm(out, lhsT, rhs, start, stop, **kw):
    return lambda e: e.matmul(out, lhsT=lhsT, rhs=rhs, start=start, stop=stop, **kw)


def tr(out, in_, ident):
    return lambda e: e.transpose(out=out, in_=in_, identity=ident)


def _slopes():
    return np.exp2(-8.0 * np.arange(1, 9, dtype=np.float64) / 8.0)


def _wfun(delta, h):
    d = np.asarray(delta, dtype=np.int64)
    c = ((d >= 0) & (d <= 128)).astype(np.float64)
    c += ((d >= 0) & (d <= 512) & (d % 4 == 0))
    c += ((d >= 0) & (d <= 2048) & (d % 16 == 0))
    return c * np.exp(-_slopes()[h] * np.maximum(d, 0))


def _consts():
    cs = {}
    cs["ident"] = np.eye(128, dtype=np.float32)
    u = np.arange(128)[:, None]
    q = np.arange(128)[None, :]
    tab = np.zeros((128, 17, 8, 128), np.float32)
    for j in range(17):
        for h in range(8):
            tab[:, j, h, :] = _wfun(128 * j + q - u, h)
    cs["tab"] = tab
    ta = np.zeros((128, 4, 4, 8), np.float32)
    for blk in range(4):
        for i in range(4):
            for h in range(8):
                ta[:, blk, i, h] = _wfun(2048 + i - (1536 + 128 * blk + np.arange(128)), h)
    cs["ta"] = ta
    tb = np.zeros((96, 4, 8), np.float32)
    for i in range(4):
        for h in range(8):
            tb[:, i, h] = _wfun(2048 - 16 * np.arange(96), h)
    cs["tb"] = tb
    tn = np.zeros((16, 16, 8), np.float32)
    for uu in range(16):
        for qq in range(16):
            if uu // 4 == qq // 4 and uu <= qq:
                for h in range(8):
                    tn[uu, qq, h] = _wfun(qq - uu, h)
    cs["tn"] = tn
    for T, L, nm in ((128, 32, "p"), (16, 4, "s")):
        s = np.arange(T)
        same = (s[:, None] // L) == (s[None, :] // L)
        cs["bt" + nm] = (same & (s[:, None] <= s[None, :])).astype(np.float32)
        cs["m2" + nm] = (same & (s[:, None] > s[None, :])).astype(np.float32)
        cs["cm" + nm] = ((s[:, None] // L) == np.arange(4)[None, :]).astype(np.float32)
    selb = np.zeros((16, 16, 128), np.float32)
    for t in range(16):
        selb[t, t, :] = 1.0
    cs["selb"] = selb
    cs["onesm"] = np.full((128, 128), 1.0 / 128.0, np.float32)
    return cs


def build_nc(stop_after="F", debug=False):
    nc = bass.Bass("TRN2", target_bir_lowering=False)
    S = Sched(nc)

    def din(name, shape):
        return nc.dram_tensor(name, list(shape), F32, kind="ExternalInput").ap()

    def dout(name, shape):
        return nc.dram_tensor(name, list(shape), F32, kind="ExternalOutput").ap()

    xc = din("xc", (NT * 128, D))
    vflag = din("vflag", (128, NT))
    hflag = din("hflag", (128, 1))
    xs = din("xs", (NS, D))
    cwk = din("cwk", (NSQ, 2048, 512))
    cwv = din("cwv", (NSQ, 2048, 512))
    shg = din("shg", (NSQ, 4, 128, 128))
    sfc = din("sfc", (NSQ * 2, DFF))
    cmk = din("cmk", (NSQ, 256, 512))
    cmv = din("cmv", (NSQ, 256, 512))
    memp = din("memp", (256, D))
    lbl = din("hg_lb_logits", (2, 512))
    norm_mix = din("norm_mix", (D,))
    w_in = din("w_in", (D, 3584))
    att_out_norm = din("att_out_norm", (512,))
    hg_out_norm = din("hg_out_norm", (512,))
    w_out = din("w_out", (D, D))
    norm_cross = din("norm_cross", (D,))
    norm_mem = din("norm_mem", (D,))
    w_cq = din("w_cq", (D, 512))
    w_ck = din("w_ck", (D, 512))
    w_cv = din("w_cv", (D, 512))
    w_co = din("w_co", (512, D))
    norm_ffn = din("norm_ffn", (D,))
    w_gate = din("w_gate", (D, DFF))
    w_up = din("w_up", (D, DFF))
    conv_w = din("conv_w", (3, DFF))
    conv_b = din("conv_b", (DFF,))
    w_down = din("w_down", (DFF, D))
    norm_final = din("norm_final", (D,))
    c_ident = din("ident", (128, 128))
    c_tab = din("tab", (128, 17, 8, 128))
    c_ta = din("ta", (128, 4, 4, 8))
    c_tb = din("tb", (96, 4, 8))
    c_tn = din("tn", (16, 16, 8))
    c_btp = din("btp", (128, 128)); c_m2p = din("m2p", (128, 128)); c_cmp = din("cmp", (128, 4))
    c_bts = din("bts", (16, 16)); c_m2s = din("m2s", (16, 16)); c_cms = din("cms", (16, 4))
    c_selb = din("selb", (16, 16, 128))
    c_onesm = din("onesm", (128, 128))

    o_y = dout("o_y", (NOUT * 128, D))
    o_wk = dout("o_wk", (NOUT * 128, 512))
    o_wv = dout("o_wv", (NOUT * 128, 512))
    o_hs = dout("o_hs", (4, 128, 128))
    o_fc = dout("o_fc", (2, DFF))
    o_mk = dout("o_mk", (256, 512))
    o_mv = dout("o_mv", (256, 512))
    o_ys = dout("o_ys", (NS, D))
    o_wks = dout("o_wks", (NS, 512))
    o_wvs = dout("o_wvs", (NS, 512))
    o_hss = dout("o_hss", (NSQ, 4, 128, 128))
    o_fcs = dout("o_fcs", (NS, DFF))
    x2d = nc.dram_tensor("d_x2" if debug else "x2d", [(NMAIN) * 128 + NS, D], F32,
                          kind="ExternalOutput" if debug else "Internal").ap()
    Bx2d = [Buf() for _ in range(NMAIN + 1)]
    Bout = Buf("out")

    es_all = contextlib.ExitStack()

    SB_RESERVE = 229376 - 212992

    def sbt(es, name, shape, dt=F32):
        a = es.enter_context(nc.sbuf_tensor("sb_" + name, list(shape), dt)).ap()
        assert nc.sbuf_bytes_remaining >= SB_RESERVE, ("SBUF budget exceeded at", name, nc.sbuf_bytes_remaining)
        return a

    banks = [nc.alloc_psum_tensor("bank%d" % i, [128, 512], F32).ap() for i in range(8)]
    Bbank = [Buf("bank%d" % i, x=True) for i in range(8)]

    class Rot:
        def __init__(self, ids):
            self.ids = ids
            self.i = 0

        def get(self):
            b = self.ids[self.i % len(self.ids)]
            self.i += 1
            return banks[b], Bbank[b]

    ident = sbt(es_all, "ident", (128, 128)); Bc = Buf("const")
    gcols = sbt(es_all, "gcols", (128, 40))
    cwc = sbt(es_all, "cwc", (128, NF, 3))
    cbc = sbt(es_all, "cbc", (128, NF))
    lbc = sbt(es_all, "lbc", (128, 2, 4))
    vfl = sbt(es_all, "vfl", (128, NT))
    hfl = sbt(es_all, "hfl", (128, 1))
    onesm = sbt(es_all, "onesm", (128, 128))
    onesb = sbt(es_all, "onesb", (128, 128), BF16)
    btp = sbt(es_all, "btp", (128, 128)); m2p = sbt(es_all, "m2p", (128, 128)); cmp_ = sbt(es_all, "cmp", (128, 4))
    bts = sbt(es_all, "bts", (16, 16)); m2s = sbt(es_all, "m2s", (16, 16)); cms = sbt(es_all, "cms", (16, 4))
    mixT = sbt(es_all, "mixT", (128, 4, NMAIN * 128), BF16)
    BmixA = [Buf() for _ in range(NMAIN)]
    BmixH = [Buf() for _ in range(NMAIN)]
    mixTs = sbt(es_all, "mixTs", (128, 4, NS), BF16)
    BmixsA = Buf(); BmixsH = Buf()
    junk = sbt(es_all, "junk", (128, D), BF16); Bjunk = Buf()
    stat = sbt(es_all, "stat", (128, 8)); Bstat = Buf()
    xn = sbt(es_all, "xn", (128, D), BF16); Bxn = Buf()
    identb = sbt(es_all, "identb", (128, 128), BF16)
    anb = sbt(es_all, "anb", (128, 256), BF16); Banb = Buf()

    cq = "c0"
    S.dma("sp", ident, c_ident, writes=[Bc], sem=cq)
    for i, g in enumerate((norm_mix, norm_cross, norm_ffn, norm_mem)):
        S.dma("sp", gcols[:, 8 * i:8 * i + 8], g.rearrange("(k p) -> p k", p=128), writes=[Bc], sem=cq,
              allow_slow_non_contiguous=True)
    S.dma("sp", gcols[:, 32:36], att_out_norm.rearrange("(k p) -> p k", p=128), writes=[Bc], sem=cq,
          allow_slow_non_contiguous=True)
    S.dma("sp", gcols[:, 36:40], hg_out_norm.rearrange("(k p) -> p k", p=128), writes=[Bc], sem=cq,
          allow_slow_non_contiguous=True)
    for j in range(3):
        S.dma("sp", cwc[:, :, j], conv_w[j].rearrange("(f p) -> p f", p=128), writes=[Bc], sem=cq,
              allow_slow_non_contiguous=True)
    S.dma("sp", cbc, conv_b.rearrange("(f p) -> p f", p=128), writes=[Bc], sem=cq, allow_slow_non_contiguous=True)
    S.dma("sp", lbc, lbl.rearrange("l (h p) -> p l h", p=128), writes=[Bc], sem=cq, allow_slow_non_contiguous=True)
    S.dma("sp", vfl, vflag, writes=[Bc], sem=cq)
    S.dma("sp", hfl, hflag, writes=[Bc], sem=cq)
    S.dma("sp", onesm, c_onesm, writes=[Bc], sem=cq)
    for dst, src in ((btp, c_btp), (m2p, c_m2p), (cmp_, c_cmp), (bts, c_bts), (m2s, c_m2s), (cms, c_cms)):
        S.dma("sp", dst, src, writes=[Bc], sem=cq)
    S.seal([Bc], cq)
    Blb = Buf("lb")
    def lb_compute(t_):
        S.op("dve", lambda e, t_=t_: e.tensor_tensor(out=t_[:, 0], in0=t_[:, 0], in1=t_[:, 1], op=ALU.subtract),
             reads=[Bc], writes=[Blb])
        S.op("act", lambda e, t_=t_: e.activation(out=t_[:, 1], in_=t_[:, 0], func=AF.Sigmoid, scale=-1.0),
             reads=[Blb], writes=[Blb])
        S.op("act", lambda e, t_=t_: e.activation(out=t_[:, 0], in_=t_[:, 0], func=AF.Sigmoid),
             reads=[Blb], writes=[Blb])

    lb_compute(lbc)
    Bones = Buf()
    S.op("dve", lambda e: e.memset(onesb, 1.0), writes=[Bones])
    S.op("dve", lambda e: e.tensor_copy(out=identb, in_=ident), reads=[Bc], writes=[Bones])

    wrot = Rot([0, 1])

    def wload(dst, src_rows, ncols, c0, bufs, sem, nk=8):
        v = src_rows.rearrange("(k p) c -> p k c", p=128)
        for k in range(nk):
            S.dma("pool", dst[:, k, :], v[:, k, c0:c0 + ncols], writes=bufs, sem=sem)

    def rstd_from_ss(T, col_in, col_out, n):
        S.op("act", lambda e: e.activation(out=stat[:T, col_out], in_=stat[:T, col_in], func=AF.Ln,
                                           scale=1.0 / n, bias=EPS), reads=[Bstat], writes=[Bstat])
        S.op("act", lambda e: e.activation(out=stat[:T, col_out], in_=stat[:T, col_out], func=AF.Exp,
                                           scale=-0.5), reads=[Bstat], writes=[Bstat])

    def norm_A(xa, Bx, T):
        S.op("act", lambda e: e.activation(out=junk[:T], in_=xa, func=AF.Square, accum_out=stat[:T, 0:1]),
             reads=[Bx], writes=[Bjunk, Bstat])
        rstd_from_ss(T, slice(0, 1), slice(1, 2), float(D))
        S.op("pool", lambda e: e.tensor_scalar(out=xn[:T], in0=xa, scalar1=stat[:T, 1:2], scalar2=1.0,
                                               op0=ALU.mult, op1=ALU.mult), reads=[Bx, Bstat], writes=[Bxn])

    def norm_B(T, gc0, dstT, BdT, rot=None):
        bk, Bb = (rot or wrot).get()
        bv = bk.bitcast(BF16).rearrange("p (a b) -> p a b", a=8)
        S.group("pe", [tr(bv[:, k, :T], xn[:T, k * 128:(k + 1) * 128], identb[:T, :T]) for k in range(8)],
                reads=[Bxn, Bones], writes=[Bb])
        S.op("dve", lambda e: e.tensor_tensor(
            out=dstT[:, 0:8, :T], in0=bv[:, :, :T],
            in1=gcols[:, gc0:gc0 + 8].unsqueeze(2).to_broadcast([128, 8, T]), op=ALU.mult),
            reads=[Bb, Bc], writes=[BdT])

    def norm_T(xa, Bx, T, gc0, dstT, BdT, rot=None):
        norm_A(xa, Bx, T)
        norm_B(T, gc0, dstT, BdT, rot)

    def proj_fm(bv, Bb, W, c0, nchunk, rhsT, T, Brd):
        fns = []
        for c in range(nchunk):
            for k in range(8):
                fns.append(mm(bv[:, c, :T], W[:, k, c0 + c * 128:c0 + (c + 1) * 128], rhsT[:, k, :T], k == 0, k == 7))
        S.group("pe", fns, reads=Brd, writes=[Bb])

    def proj_tm(bk, Bb, W, c0, ncol, lhsT, T, Brd, nk=8):
        fns = [mm(bk[:T, :ncol], lhsT[:, k, :T], W[:, k, c0:c0 + ncol], k == 0, k == nk - 1) for k in range(nk)]
        S.group("pe", fns, reads=Brd, writes=[Bb])

    def headnorm_tm(T, o_sb, Bo, nh, dh, dst, Bdst, es):
        sq = hn_sq[:T, :nh * dh].rearrange("p (h d) -> p h d", h=nh)
        S.op("pool", lambda e: e.tensor_tensor(out=sq, in0=o_sb, in1=o_sb, op=ALU.mult), reads=[Bo], writes=[Bhnsq])
        S.op("dve", lambda e: e.tensor_reduce(out=stat8[:T, 0:nh], in_=sq, axis=AX.X, op=ALU.add),
             reads=[Bhnsq], writes=[Bstat8])
        S.op("act", lambda e: e.activation(out=stat8[:T, 0:nh], in_=stat8[:T, 0:nh], func=AF.Ln, scale=1.0 / dh,
                                           bias=EPS), reads=[Bstat8], writes=[Bstat8])
        S.op("act", lambda e: e.activation(out=stat8[:T, 0:nh], in_=stat8[:T, 0:nh], func=AF.Exp, scale=-0.5),
             reads=[Bstat8], writes=[Bstat8])
        S.op("dve", lambda e: e.tensor_tensor(out=dst, in0=o_sb,
                                              in1=stat8[:T, 0:nh].unsqueeze(2).to_broadcast([T, nh, dh]),
                                              op=ALU.mult), reads=[Bo, Bstat8], writes=[Bdst])

    hn_sq = sbt(es_all, "hn_sq", (128, 512)); Bhnsq = Buf()
    stat8 = sbt(es_all, "stat8", (128, 16)); Bstat8 = Buf()
    on_sb = sbt(es_all, "on_sb", (128, 512)); Bon = Buf()
    an_sb = sbt(es_all, "an_sb", (128, 512)); Ban = Buf()

    def attn_finish(T, obanks, Bob, nh, dh, gc0, dstT, BdT, par=False):
        hh = nh // 2
        for g in range(2):
            S.op("dve", lambda e, g=g: e.tensor_scalar(out=stat8[:T, 8 + g * hh:8 + (g + 1) * hh],
                                                       in0=obanks[g][:T, :, dh], scalar1=1e-30, scalar2=None,
                                                       op0=ALU.add), reads=[Bob[g]], writes=[Bstat8])
        S.op("dve", lambda e: e.reciprocal(out=stat8[:T, 8:8 + nh], in_=stat8[:T, 8:8 + nh]),
             reads=[Bstat8], writes=[Bstat8])
        onv = on_sb[:T, :nh * dh].rearrange("p (h d) -> p h d", h=nh)
        for g in range(2):
            S.op("dve", lambda e, g=g: e.tensor_tensor(
                out=(onv[:, g:nh:2, :] if par else onv[:, g * hh:(g + 1) * hh, :]), in0=obanks[g][:T, :, 0:dh],
                in1=stat8[:T, 8 + g * hh:8 + (g + 1) * hh].unsqueeze(2).to_broadcast([T, hh, dh]), op=ALU.mult),
                reads=[Bob[g], Bstat8], writes=[Bon])
        return onv

    def tm_to_fm(T, src, Bsrc, nchunk, gc0, dstT, BdT, dt_scale=True):
        bk, Bb = wrot.get()
        bv = bk.rearrange("p (a b) -> p a b", a=4)
        S.group("pe", [tr(bv[:, c, :T], src[:T, c * 128:(c + 1) * 128], ident[:T, :T]) for c in range(nchunk)],
                reads=[Bsrc, Bc], writes=[Bb])
        if gc0 is None:
            S.op("act", lambda e: e.activation(out=dstT[:, 0:nchunk, :T], in_=bv[:, 0:nchunk, :T], func=AF.Copy),
                 reads=[Bb], writes=[BdT])
        else:
            S.op("dve", lambda e: e.tensor_tensor(
                out=dstT[:, 0:nchunk, :T], in0=bv[:, 0:nchunk, :T],
                in1=gcols[:, gc0:gc0 + nchunk].unsqueeze(2).to_broadcast([128, nchunk, T]), op=ALU.mult),
                reads=[Bb, Bc], writes=[BdT])

    esA = contextlib.ExitStack()
    Wqkv = sbt(esA, "Wqkv", (128, 8, 1536), BF16); BWqkv = Buf()
    wload(Wqkv, w_in, 1536, 0, [BWqkv], "wA")
    S.seal([BWqkv], "wA")
    xt = [sbt(esA, "xtA%d" % i, (128, D)) for i in range(2)]; Bxt = [Buf(), Buf()]
    hT = [sbt(esA, "hTA%d" % i, (128, 8, 128), BF16) for i in range(2)]; BhT = [Buf(), Buf()]

    esA1 = contextlib.ExitStack()
    kst = [sbt(esA1, "kst0", (128, 512))] * 2; Bkst = [Buf()] * 2
    vst = [sbt(esA1, "vst0", (128, 512))] * 2; Bvst = [Buf()] * 2
    KT = sbt(esA1, "KT", (128, 4, NT * 128), BF16); BKT = [Buf() for _ in range(NT)]
    V = sbt(esA1, "V", (128, NT, 8, 65), BF16); BV = [Buf() for _ in range(NT)]
    tab = sbt(esA1, "tab", (128, 17, 8, 128), BF16); Btab = Buf()
    for j in range(17):
        S.dma("pool", tab[:, j], c_tab[:, j], writes=[Btab], sem="tab")
    S.seal([Btab], "tab")
    QTg = [sbt(esA1, "QTg0", (128, 4, 512), BF16)] * 2; BQTg = [Buf()] * 2
    DEPTH = 4
    E = [sbt(esA1, "E%d" % i, (128, 512), BF16) for i in range(DEPTH)]; BE = [Buf() for _ in range(DEPTH)]
    PT = [sbt(esA1, "PT%d" % i, (128, 512), BF16) for i in range(DEPTH)]; BPT = [Buf() for _ in range(DEPTH)]
    osb = [sbt(esA1, "osb%d" % i, (128, 512)) for i in range(4)]; Bosb = [Buf() for _ in range(4)]
    wrot.ids = [0, 1, 2, 3]
    srot = wrot
    otb = [banks[4 + i] for i in range(4)]; Botb = [Bbank[4 + i] for i in range(4)]
    S.dma("sp", xt[0], xc[0:128, :], writes=[Bxt[0]], sem="xA0")
    S.dma("sp", xt[1], xc[128:256, :], writes=[Bxt[1]], sem="xA1")

    def tileA(ti, qdst, Bq):
        sl = ti % 2
        outt = ti > T0
        norm_B(128, 0, hT[sl], BhT[sl])
        if ti + 2 < NT:
            S.dma("sp", xt[sl], xc[(ti + 2) * 128:(ti + 3) * 128, :], writes=[Bxt[sl]], sem="xA%d" % sl)
        if ti + 1 < NT:
            norm_A(xt[1 - sl], Bxt[1 - sl], 128)
        bk, Bb = wrot.get(); bv = bk.rearrange("p (a b) -> p a b", a=4)
        proj_fm(bv, Bb, Wqkv, 512, 4, hT[sl], 128, [BWqkv, BhT[sl]])
        S.op("act", lambda e: e.activation(out=KT[:, :, ti * 128:(ti + 1) * 128], in_=bv, func=AF.Copy),
             reads=[Bb], writes=[BKT[ti]])
        bk, Bb = wrot.get()
        proj_tm(bk, Bb, Wqkv, 1024, 512, hT[sl], 128, [BWqkv, BhT[sl]])
        S.op("dve", lambda e: e.tensor_copy(out=V[:, ti, :, 0:64], in_=bk.rearrange("p (h d) -> p h d", h=8)),
             reads=[Bb], writes=[BV[ti]])
        S.op("pool", lambda e: e.tensor_copy(out=V[:, ti, :, 64:65],
                                             in_=vfl[:, ti:ti + 1].unsqueeze(1).to_broadcast([128, 8, 1])),
             reads=[Bc], writes=[BV[ti]])
        if outt:
            so = ti % 2
            S.op("act", lambda e: e.activation(out=vst[so], in_=bk, func=AF.Copy), reads=[Bb], writes=[Bvst[so]])
            S.dma("sp", o_wv[(ti - 16) * 128:(ti - 15) * 128, :], vst[so], reads=[Bvst[so]], sem="ov")
            bk2, Bb2 = wrot.get()
            proj_tm(bk2, Bb2, Wqkv, 512, 512, hT[sl], 128, [BWqkv, BhT[sl]])
            S.op("act", lambda e: e.activation(out=kst[so], in_=bk2, func=AF.Copy), reads=[Bb2], writes=[Bkst[so]])
            S.dma("sp", o_wk[(ti - 16) * 128:(ti - 15) * 128, :], kst[so], reads=[Bkst[so]], sem="ok")
        if qdst is not None:
            bk3, Bb3 = wrot.get(); bv3 = bk3.rearrange("p (a b) -> p a b", a=4)
            proj_fm(bv3, Bb3, Wqkv, 0, 4, hT[sl], 128, [BWqkv, BhT[sl]])
            S.op("act", lambda e: e.activation(out=qdst, in_=bv3, func=AF.Copy), reads=[Bb3], writes=[Bq])

    def finish4(T, bank3, Bbk, hg, dstT, BdT):
        S.op("dve", lambda e: e.tensor_scalar(out=stat8[:T, 8:12], in0=bank3[:T, :, 64], scalar1=1e-30, scalar2=None,
                                              op0=ALU.add), reads=[Bbk], writes=[Bstat8])
        S.op("dve", lambda e: e.reciprocal(out=stat8[:T, 8:12], in_=stat8[:T, 8:12]), reads=[Bstat8], writes=[Bstat8])
        onv = on_sb[:T, 0:256].rearrange("p (h d) -> p h d", h=4)
        S.op("dve", lambda e: e.tensor_tensor(out=onv, in0=bank3[:T, :, 0:64],
                                              in1=stat8[:T, 8:12].unsqueeze(2).to_broadcast([T, 4, 64]), op=ALU.mult),
             reads=[Bbk, Bstat8], writes=[Bon])
        anv = anb[:T, 0:256].rearrange("p (h d) -> p h d", h=4)
        headnorm_tm(T, onv, Bon, 4, 64, anv, Banb, None)
        bk, Bb = wrot.get(); bv = bk.bitcast(BF16).rearrange("p (a b) -> p a b", a=8)
        S.group("pe", [tr(bv[:, c, :T], anb[:T, c * 128:(c + 1) * 128], identb[:T, :T]) for c in range(2)],
                reads=[Banb, Bones], writes=[Bb])
        S.op("dve", lambda e: e.tensor_tensor(
            out=dstT[:, 2 * hg:2 * hg + 2, :T], in0=bv[:, 0:2, :T],
            in1=gcols[:, 32 + 2 * hg:34 + 2 * hg].unsqueeze(2).to_broadcast([128, 2, T]), op=ALU.mult),
            reads=[Bb, Bc], writes=[BdT])

    if stop_after == "C":
        S.finish()
        return nc, S
    norm_A(xt[0], Bxt[0], 128)
    for ti in range(T0):
        tileA(ti, None, None)
    groups = [[T0]] + [list(range(t, t + 4)) for t in range(T0 + 1, NT, 4)]
    step = 0
    for gi_, grp in enumerate(groups):
        qs = gi_ % 2
        nt = len(grp); t0 = grp[0]; N = nt * 128
        for m, ti in enumerate(grp):
            tileA(ti, QTg[qs][:, :, m * 128:(m + 1) * 128], BQTg[qs])
        kb_lo = max(0, t0 - 16); kb_hi = t0 + nt - 1
        for hg in range(2):
            firstw = [True] * 4
            steps = []
            for kb in range(kb_lo, kb_hi + 1):
                m_lo = max(0, kb - t0); m_hi = min(nt - 1, kb + 16 - t0)
                for hh in range(4):
                    steps.append((kb, hh, m_lo, m_hi))
            sbanks = {}

            def emit_S(n):
                kb, hh, m_lo, m_hi = steps[n]
                h = 4 * hg + hh; c = h // 2; po = (h % 2) * 64
                c0_, c1_ = m_lo * 128, (m_hi + 1) * 128
                sbk, Bs = srot.get()
                sbanks[n] = (sbk, Bs)
                S.group("pe", [mm(sbk[:, c0_:c1_], KT[po:po + 64, c, kb * 128:(kb + 1) * 128],
                                  QTg[qs][po:po + 64, c, c0_:c1_], True, True)],
                        reads=[BKT[kb], BQTg[qs]], writes=[Bs])

            def emit_rest(n):
                kb, hh, m_lo, m_hi = steps[n]
                h = 4 * hg + hh
                nm = m_hi - m_lo + 1
                c0_, c1_ = m_lo * 128, (m_hi + 1) * 128
                j_lo = t0 + m_lo - kb
                es_ = n % DEPTH
                sbk, Bs = sbanks.pop(n)
                S.op("act", lambda e: e.activation(out=E[es_][:, c0_:c1_], in_=sbk[:, c0_:c1_], func=AF.Exp, scale=0.125),
                     reads=[Bs], writes=[BE[es_]])
                S.op("dve",
                     lambda e: e.tensor_tensor(out=PT[es_][:, c0_:c1_].rearrange("p (m q) -> p m q", m=nm),
                                               in0=E[es_][:, c0_:c1_].rearrange("p (m q) -> p m q", m=nm),
                                               in1=tab[:, j_lo:j_lo + nm, h, :], op=ALU.mult),
                     reads=[BE[es_], Btab], writes=[BPT[es_]])
                S.group("pe", [mm(otb[hh][:65, c0_:c1_], V[:, kb, h, :], PT[es_][:, c0_:c1_], firstw[hh],
                                  kb == kb_hi, skip_group_check=True)],
                        reads=[BPT[es_], BV[kb]], writes=[Botb[hh]])
                firstw[hh] = False

            ns_ = len(steps)
            for n in range(min(2, ns_)):
                emit_S(n)
            for p in range(0, ns_, 2):
                for n in (p + 2, p + 3):
                    if n < ns_:
                        emit_S(n)
                for n in (p, p + 1):
                    if n < ns_:
                        emit_rest(n)
            for hh in range(4):
                S.op("act", lambda e, hh=hh: e.activation(out=osb[hh][:65, :N], in_=otb[hh][:65, :N], func=AF.Copy),
                     reads=[Botb[hh]], writes=[Bosb[hh]])
            for m, ti in enumerate(grp):
                bk, Bb = srot.get()
                b3 = bk[:, 0:260].rearrange("p (h d) -> p h d", h=4)
                S.group("pe", [tr(b3[:, hh, :], osb[hh][:65, m * 128:(m + 1) * 128], ident[:65, :65]) for hh in range(4)],
                        reads=Bosb + [Bc], writes=[Bb])
                mt = ti - T0
                finish4(128, b3, Bb, hg, mixT[:, :, mt * 128:(mt + 1) * 128], BmixA[mt])
    S.barrier()
    esA1.close()
    zs = sbt(esA, "zs", (NS, 1536)); Bzs = Buf()
    hsT = sbt(esA, "hsT", (128, 8, NS), BF16); BhsT = Buf()
    xst = sbt(esA, "xst", (NS, D)); Bxst = Buf()
    S.dma("sp", xst, xs, writes=[Bxst], sem="xs")
    norm_T(xst, Bxst, NS, 0, hsT, BhsT)
    for n in range(3):
        bk, Bb = wrot.get()
        proj_tm(bk, Bb, Wqkv, n * 512, 512, hsT, NS, [BWqkv, BhsT])
        S.op("act", lambda e, bk=bk, n=n: e.activation(out=zs[:, n * 512:(n + 1) * 512], in_=bk[:NS, :], func=AF.Copy),
             reads=[Bb], writes=[Bzs])
    S.dma("pool", o_wks, zs[:, 512:1024], reads=[Bzs], sem="osmall")
    S.dma("pool", o_wvs, zs[:, 1024:1536], reads=[Bzs], sem="osmall")

    if stop_after == "A1":
        S.finish()
        return nc, S

    esA2 = contextlib.ExitStack()
    ta = sbt(esA2, "ta", (128, 4, 4, 8)); tb_ = sbt(esA2, "tb", (96, 4, 8)); tn = sbt(esA2, "tn", (16, 16, 8))
    selb = sbt(esA2, "selb", (16, 16, 128))
    Bt2 = Buf()
    S.dma("sp", selb, c_selb, writes=[Bt2], sem="c2")
    S.dma("sp", ta, c_ta, writes=[Bt2], sem="c2"); S.dma("sp", tb_, c_tb, writes=[Bt2], sem="c2")
    S.dma("sp", tn, c_tn, writes=[Bt2], sem="c2"); S.seal([Bt2], "c2")
    KA = [sbt(esA2, "KA%d" % i, (128, 4, 512)) for i in range(2)]
    VA = [sbt(esA2, "VA%d" % i, (128, 4, 8, 65)) for i in range(2)]
    KB = [sbt(esA2, "KB%d" % i, (96, 4, 512)) for i in range(2)]
    VB = [sbt(esA2, "VB%d" % i, (96, 4, 8, 65)) for i in range(2)]
    BKA = [Buf(), Buf()]; BVA = [Buf(), Buf()]; BKB = [Buf(), Buf()]; BVB = [Buf(), Buf()]
    for i in range(2):
        S.op("pool", lambda e, i=i: e.memset(VA[i], 1.0), writes=[BVA[i]])
        S.op("pool", lambda e, i=i: e.memset(VB[i], 1.0), writes=[BVB[i]])
    Vn = sbt(esA2, "Vn", (NS, 8, 65)); BVn = Buf()
    S.op("pool", lambda e: e.memset(Vn, 1.0), writes=[BVn])
    S.op("dve", lambda e: e.tensor_copy(out=Vn[:, :, 0:64], in_=zs[:, 1024:1536].rearrange("p (h d) -> p h d", h=8)),
         reads=[Bzs], writes=[BVn])
    prodA = sbt(esA2, "prodA", (128, 4, 512)); BprodA = Buf()
    prodB = sbt(esA2, "prodB", (128, 512)); BprodB = Buf()
    SCA = sbt(esA2, "SCA", (128, 4, 4, 8)); BSCA = Buf()
    SCB = sbt(esA2, "SCB", (96, 4, 8)); BSCB = Buf()
    SCN = sbt(esA2, "SCN", (NS, NS, 8)); BSCN = Buf()
    PAz = [sbt(esA2, "PAz%d" % i, (128, 4, NS, 8)) for i in range(NSQ)]; BPAz = [Buf() for _ in range(NSQ)]
    PBz = [sbt(esA2, "PBz%d" % i, (96, 4, NS, 8)) for i in range(NSQ)]; BPBz = [Buf() for _ in range(NSQ)]
    PN = sbt(esA2, "PN", (NS, NS, 8)); BPN = Buf()
    for i in range(NSQ):
        S.op("pool", lambda e, i=i: e.memset(PAz[i], 0.0), writes=[BPAz[i]])
        S.op("pool", lambda e, i=i: e.memset(PBz[i], 0.0), writes=[BPBz[i]])
    Bob = [Bbank[6], Bbank[7]]
    obs = [banks[6][:NS, 0:260].rearrange("p (h d) -> p h d", h=4),
           banks[7][:NS, 0:260].rearrange("p (h d) -> p h d", h=4)]
    first = [True, True]

    def pv(lhsT, rhs_of_h, Brd):
        for g in range(2):
            fns = []
            for hh in range(4):
                fns.append(mm(obs[g][:, hh, :], lhsT(4 * g + hh), rhs_of_h(4 * g + hh), first[g] and hh == 0, False,
                              skip_group_check=True))
            first[g] = False
            S.group("pe", fns, reads=Brd, writes=[Bob[g]])

    for s in range(NSQ):
        sl = s % 2
        S.dma("sp", KA[sl], cwk[s, 1536:2048, :].rearrange("(b p) c -> p b c", p=128), writes=[BKA[sl]], sem="ka%d" % sl)
        for blk in range(4):
            S.dma("sp", VA[sl][:, blk, :, 0:64],
                  cwv[s, 1536 + 128 * blk:1664 + 128 * blk, :].rearrange("p (h d) -> p h d", h=8),
                  writes=[BVA[sl]], sem="va%d" % sl)
        S.dma("sp", KB[sl], cwk[s, 0:1536, :].rearrange("(a r) c -> a r c", r=16)[:, 0:4, :], writes=[BKB[sl]],
              sem="kb%d" % sl)
        for r in range(4):
            S.dma("sp", VB[sl][:, r, :, 0:64],
                  cwv[s, 0:1536, :].rearrange("(a r) (h d) -> a r h d", r=16, h=8)[:, r], writes=[BVB[sl]],
                  sem="vb%d" % sl)
        for i in range(4):
            tq = 4 * s + i
            bk, Bb = wrot.get()
            S.group("pe", [mm(bk[:, :], selb[:, tq, :], zs[:, 0:512], True, True)], reads=[Bt2, Bzs], writes=[Bb])
            S.op("dve", lambda e, bk=bk, sl=sl: e.tensor_tensor(
                out=prodA, in0=KA[sl], in1=bk.unsqueeze(1).to_broadcast([128, 4, 512]), op=ALU.mult),
                reads=[BKA[sl], Bb], writes=[BprodA])
            S.op("dve", lambda e, i=i: e.tensor_reduce(
                out=SCA[:, :, i, :], in_=prodA.rearrange("p b (h d) -> p b h d", h=8), axis=AX.X, op=ALU.add),
                reads=[BprodA], writes=[BSCA])
            S.op("dve", lambda e, bk=bk, sl=sl, i=i: e.tensor_tensor(
                out=prodB[:96], in0=KB[sl][:, i, :], in1=bk[:96, :], op=ALU.mult),
                reads=[BKB[sl], Bb], writes=[BprodB])
            S.op("dve", lambda e, i=i: e.tensor_reduce(
                out=SCB[:, i, :], in_=prodB[:96].rearrange("p (h d) -> p h d", h=8), axis=AX.X, op=ALU.add),
                reads=[BprodB], writes=[BSCB])
            S.op("dve", lambda e, bk=bk: e.tensor_tensor(
                out=prodB[:NS], in0=zs[:, 512:1024], in1=bk[:NS, :], op=ALU.mult),
                reads=[Bzs, Bb], writes=[BprodB])
            S.op("dve", lambda e, tq=tq: e.tensor_reduce(
                out=SCN[:, tq, :], in_=prodB[:NS].rearrange("p (h d) -> p h d", h=8), axis=AX.X, op=ALU.add),
                reads=[BprodB], writes=[BSCN])
        S.op("act", lambda e: e.activation(out=SCA, in_=SCA, func=AF.Exp, scale=0.125), reads=[BSCA], writes=[BSCA])
        S.op("dve", lambda e, s=s: e.tensor_tensor(out=PAz[s][:, :, 4 * s:4 * s + 4, :], in0=SCA, in1=ta, op=ALU.mult),
             reads=[BSCA, Bt2], writes=[BPAz[s]])
        S.op("act", lambda e: e.activation(out=SCB, in_=SCB, func=AF.Exp, scale=0.125), reads=[BSCB], writes=[BSCB])
        for i in range(4):
            S.op("dve", lambda e, s=s, i=i: e.tensor_tensor(out=PBz[s][:, i, 4 * s + i, :], in0=SCB[:, i, :],
                                                            in1=tb_[:, i, :], op=ALU.mult),
                 reads=[BSCB, Bt2], writes=[BPBz[s]])
        for blk in range(4):
            pv(lambda h, blk=blk, s=s: PAz[s][:, blk, :, h], lambda h, blk=blk, sl=sl: VA[sl][:, blk, h, :],
               [BPAz[s], BVA[sl]])
            pv(lambda h, blk=blk, s=s: PBz[s][:, blk, :, h], lambda h, blk=blk, sl=sl: VB[sl][:, blk, h, :],
               [BPBz[s], BVB[sl]])
    S.op("act", lambda e: e.activation(out=SCN, in_=SCN, func=AF.Exp, scale=0.125), reads=[BSCN], writes=[BSCN])
    S.op("dve", lambda e: e.tensor_tensor(out=PN, in0=SCN, in1=tn, op=ALU.mult), reads=[BSCN, Bt2], writes=[BPN])
    pv(lambda h: PN[:, :, h], lambda h: Vn[:, h, :], [BPN, BVn])
    onv = attn_finish(NS, obs, Bob, 8, 64, 32, None, None)
    anv = an_sb[:NS, :].rearrange("p (h d) -> p h d", h=8)
    headnorm_tm(NS, onv, Bon, 8, 64, anv, Ban, None)
    tm_to_fm(NS, an_sb, Ban, 4, 32, mixTs[:, 0:4, :], BmixsA)
    S.barrier()
    esA2.close()
    esA.close()

    if debug:
        d_mix = nc.dram_tensor("d_mix", [128, 4, NMAIN * 128], BF16, kind="ExternalOutput").ap()
        d_mixs = nc.dram_tensor("d_mixs", [128, 4, NS], BF16, kind="ExternalOutput").ap()
        S.dma("sp", d_mix, mixT, reads=BmixA, sem="dbg")
        S.dma("sp", d_mixs, mixTs, reads=[BmixsA], sem="dbg")
    if stop_after == "A":
        S.finish()
        return nc, S


    esH = contextlib.ExitStack()
    mixH = sbt(esH, "mixH", (128, 4, NMAIN * 128), BF16)
    mixHs = sbt(esH, "mixHs", (128, 4, NS), BF16)
    lbb = sbt(esH, "lbb", (128, 2, 512)); BlbH = Buf()
    S.dma("sp", lbb, lbl.partition_broadcast(128), writes=[Bc2 := Buf()], sem="cH")
    S.seal([Bc2], "cH")
    S.op("dve", lambda e: e.tensor_tensor(out=lbb[:, 0], in0=lbb[:, 0], in1=lbb[:, 1], op=ALU.subtract),
         reads=[Bc2], writes=[BlbH])
    S.op("act", lambda e: e.activation(out=lbb[:, 1], in_=lbb[:, 0], func=AF.Sigmoid, scale=-1.0),
         reads=[BlbH], writes=[BlbH])
    S.op("act", lambda e: e.activation(out=lbb[:, 0], in_=lbb[:, 0], func=AF.Sigmoid), reads=[BlbH], writes=[BlbH])
    Wh = sbt(esH, "Wh", (128, 8, 2048), BF16); BWh = Buf()
    wload(Wh, w_in, 2048, 1536, [BWh], "wH")
    Wout = sbt(esH, "Wout", (128, 8, D), BF16)
    wload(Wout, w_out, D, 0, [BWh], "wH")
    Wcq = sbt(esH, "Wcq", (128, 8, 512), BF16)
    wload(Wcq, w_cq, 512, 0, [BWh], "wH")
    Wco = sbt(esH, "Wco", (128, 4, D), BF16)
    wload(Wco, w_co, D, 0, [BWh], "wH", nk=4)
    MKT = sbt(esH, "MKT", (128, 4, 256), BF16); BMKT = Buf()
    MV = sbt(esH, "MV", (128, 2, 512), BF16); BMV = Buf()
    xt = [sbt(esH, "xtH%d" % i, (128, D)) for i in range(3)]; Bxt = [Buf(), Buf(), Buf()]
    hT = [sbt(esH, "hTH%d" % i, (128, 8, 128), BF16) for i in range(2)]; BhT = [Buf(), Buf()]
    stg = [sbt(esH, "stg%d" % i, (128, 512)) for i in range(2)]; Bstg = [Buf(), Buf()]
    hrot = Rot([0, 1, 2])
    wrot.ids = [0, 1, 2]
    orot = Rot([6])
    brot = Rot([3, 4, 5])
    obrot = Rot([7])
    esM = contextlib.ExitStack()
    Wck = sbt(esM, "Wck", (128, 8, 512), BF16); Wcv = sbt(esM, "Wcv", (128, 8, 512), BF16)
    wload(Wck, w_ck, 512, 0, [BWh], "wH"); wload(Wcv, w_cv, 512, 0, [BWh], "wH")
    S.seal([BWh], "wH")
    nst = 0
    for mt in range(2):
        sl = mt % 2
        S.dma("sp", xt[sl], memp[mt * 128:(mt + 1) * 128, :], writes=[Bxt[sl]], sem="xH%d" % sl)
        norm_T(xt[sl], Bxt[sl], 128, 24, hT[sl], BhT[sl])
        bk, Bb = hrot.get(); bv = bk.rearrange("p (a b) -> p a b", a=4)
        proj_fm(bv, Bb, Wck, 0, 4, hT[sl], 128, [BWh, BhT[sl]])
        S.op("act", lambda e, bv=bv, mt=mt: e.activation(out=MKT[:, :, mt * 128:(mt + 1) * 128], in_=bv, func=AF.Copy),
             reads=[Bb], writes=[BMKT])
        for W_, dst in ((Wck, o_mk), (Wcv, o_mv)):
            bk, Bb = hrot.get()
            proj_tm(bk, Bb, W_, 0, 512, hT[sl], 128, [BWh, BhT[sl]])
            so = nst % 2; nst += 1
            S.op("act", lambda e, bk=bk, so=so: e.activation(out=stg[so], in_=bk, func=AF.Copy),
                 reads=[Bb], writes=[Bstg[so]])
            S.dma("pool", dst[mt * 128:(mt + 1) * 128, :], stg[so], reads=[Bstg[so]], sem="om%d" % so)
            if W_ is Wcv:
                S.op("dve", lambda e, bk=bk, mt=mt: e.tensor_copy(out=MV[:, mt, :], in_=bk), reads=[Bb], writes=[BMV])
    S.barrier()
    esM.close()

    wbes = [esH]

    def wb(name, shape, dt=F32):
        return sbt(wbes[0], name, shape, dt), Buf()
    cT, BcT = wb("cT", (128, 8, 128), BF16); cqT, BcqT = wb("cqT", (128, 4, 128), BF16)
    Pc = [wb("Pc%d" % i, (128, 4, 128), BF16) for i in range(2)]
    coT, BcoT = wb("coT", (128, 4, 128), BF16); rden, Brden = stg[0], Bstg[0]
    t_xb, Bt_xb = wb("t_xb", (128, 512), BF16)
    hsT = sbt(esH, "hsTH", (128, 8, NS), BF16); BhsT = Buf()
    esHw = contextlib.ExitStack()
    wbes[0] = esHw
    t_f, Bt_f = wb("t_f", (128, 512)); t_g, Bt_g = wb("t_g", (128, 512)); t_kk, Bt_kk = wb("t_kk", (128, 512))
    t_er, Bt_er = wb("t_er", (128, 512)); t_v, Bt_v = wb("t_v", (128, 512), BF16)
    khm = [wb("khm%d" % c, (128, 512), BF16) for c in range(4)]
    t_eb, Bt_eb = wb("t_eb", (128, 4, 128)); t_enb, Bt_enb = wb("t_enb", (128, 4, 128))
    t_q, Bt_q = wb("t_q", (128, 4, 128), BF16); t_sg, Bt_sg = wb("t_sg", (128, 4, 128), BF16)
    t_x, Bt_x = wb("t_x", (128, 512), BF16)
    t_qb, Bt_qb = wb("t_qb", (128, 512), BF16)
    t_gs, Bt_gs = wb("t_gs", (128, 4, 128), BF16)
    t_am, Bt_am = wb("t_am", (128, 4, 128), BF16); t_osq, Bt_osq = t_am, Bt_am
    t_rs, Bt_rs = t_er.rearrange("p (a b) -> p a b", a=4), Bt_er
    t_sgg, Bt_sgg = t_gs, Bt_gs
    NR = 5
    ring = [wb("Sring%d" % i, (128, 4, 128)) for i in range(NR)]
    ringb = [wb("Sringb%d" % i, (128, 4, 128), BF16) for i in range(NR)]
    Sos = [wb("Sos%d" % i, (128, 4, 128)) for i in range(2)]
    S.op("pool", lambda e: e.memset(ring[0][0], 0.0), writes=[ring[0][1]])
    S.op("pool", lambda e: e.memset(ringb[0][0], 0.0), writes=[ringb[0][1]])
    rcur = [0]

    def hgrn_tile(T, L, hTa, BhTa, masks, full, chain, mixdst, Bmixdst, s0list=None, sout=None, on_state=None,
                  need_shadow=False):
        BTm, M2m, CMm = masks
        yield
        bk, Bb = hrot.get()
        yield
        proj_tm(bk, Bb, Wh, 512, 512, hTa, T, [BWh, BhTa])
        yield
        S.op("act", lambda e: e.activation(out=t_f[:T], in_=bk[:T, :], func=AF.Sigmoid), reads=[Bb], writes=[Bt_f])
        yield
        bk2, Bb2 = hrot.get()
        yield
        proj_tm(bk2, Bb2, Wh, 1024, 512, hTa, T, [BWh, BhTa])
        yield
        S.op("act", lambda e: e.activation(out=t_v[:T], in_=bk2[:T, :], func=AF.Copy), reads=[Bb2], writes=[Bt_v])
        yield
        S.op("dve", lambda e: e.tensor_tensor(out=t_f[:T], in0=t_f[:T], in1=lbb[:T, 1], op=ALU.mult),
             reads=[BlbH], writes=[Bt_f])
        yield
        S.op("dve", lambda e: e.tensor_tensor(out=t_f[:T], in0=t_f[:T], in1=lbb[:T, 0], op=ALU.add),
             reads=[BlbH], writes=[Bt_f])
        yield
        S.op("act", lambda e: e.activation(out=t_g[:T], in_=t_f[:T], func=AF.Ln), reads=[Bt_f], writes=[Bt_g])
        yield
        S.op("pool", lambda e: e.tensor_scalar(out=t_kk[:T], in0=t_f[:T], scalar1=-1.0, scalar2=1.0, op0=ALU.mult,
                                               op1=ALU.add), reads=[Bt_f], writes=[Bt_kk])
        yield
        bk, Bb = hrot.get()
        yield
        S.group("pe", [mm(bk[:T, :], M2m[:T, :T], t_g[:T, :], True, True)], reads=[Bc, Bt_g], writes=[Bb])
        yield
        S.op("act", lambda e: e.activation(out=t_er[:T], in_=bk[:T, :], func=AF.Exp), reads=[Bb], writes=[Bt_er])
        yield
        if full:
            tv = lambda a: a.rearrange("p a b -> p (a b)")
            bkb, Bbb = hrot.get()
            S.group("pe", [mm(bkb[:T, :], BTm[:T, :T], t_g[:T, :], True, True)], reads=[Bc, Bt_g], writes=[Bbb])
            S.op("act", lambda e: e.activation(out=tv(t_enb)[:T], in_=bkb[:T, :], func=AF.Exp, scale=-1.0),
                 reads=[Bbb], writes=[Bt_enb])
            S.op("act", lambda e: e.activation(out=t_f[:T], in_=bkb[:T, :], func=AF.Exp), reads=[Bbb], writes=[Bt_f])
            S.op("dve", lambda e: e.tensor_tensor(out=t_x[:T], in0=t_kk[:T], in1=tv(t_enb)[:T], op=ALU.mult),
                 reads=[Bt_kk, Bt_enb], writes=[Bt_x])
        yield
        S.op("dve", lambda e: e.tensor_tensor(out=t_kk[:T], in0=t_kk[:T], in1=t_er[:T], op=ALU.mult),
             reads=[Bt_er], writes=[Bt_kk])
        yield
        for c in range(4):
            if c % 2 == 0:
                S.op("act", lambda e, c=c: e.activation(out=khm[c][0][:T], in_=t_kk[:T], func=AF.Copy,
                                                        scale=CMm[:T, c:c + 1]),
                     reads=[Bt_kk, Bc], writes=[khm[c][1]])
            else:
                S.op("pool", lambda e, c=c: e.tensor_scalar(out=khm[c][0][:T], in0=t_kk[:T], scalar1=CMm[:T, c:c + 1],
                                                            scalar2=1.0, op0=ALU.mult, op1=ALU.mult),
                     reads=[Bt_kk, Bc], writes=[khm[c][1]])
        yield
        bk, Bb = hrot.get(); bvb = bk.rearrange("p (a b) -> p a b", a=4)
        yield
        S.group("pe", [mm(bvb[:, hd, :T], t_g[:T, hd * 128:(hd + 1) * 128], BTm[:T, :T], True, True) for hd in range(4)],
                reads=[Bt_g, Bc], writes=[Bb])
        yield
        S.op("act", lambda e: e.activation(out=t_eb[:, :, :T], in_=bvb[:, :, :T], func=AF.Exp), reads=[Bb], writes=[Bt_eb])
        slots = []
        yield
        for c in range(4):
            if chain:
                cur = ring[rcur[0] % NR]; nxt = ring[(rcur[0] + 1) % NR]
                curb = ringb[rcur[0] % NR]; nxtb = ringb[(rcur[0] + 1) % NR]; rcur[0] += 1
            else:
                cur = s0list[c]; nxt = sout[c]
                curb = ringb[c]; nxtb = None
                S.op("pool", lambda e, cur=cur, curb=curb: e.tensor_copy(out=curb[0], in_=cur[0]),
                     reads=[cur[1]], writes=[curb[1]])
            slots.append(curb)
            bk, Bb = hrot.get(); bvk = bk.rearrange("p (a b) -> p a b", a=4)
            S.group("pe", [mm(bvk[:, hd, :], khm[c][0][:T, hd * 128:(hd + 1) * 128], t_v[:T, hd * 128:(hd + 1) * 128],
                              True, True) for hd in range(4)], reads=[khm[c][1], Bt_v], writes=[Bb])
            for hd in range(4):
                S.op("dve", lambda e, hd=hd, c=c, cur=cur, nxt=nxt, bvk=bvk: e.scalar_tensor_tensor(
                    out=nxt[0][:, hd, :], in0=cur[0][:, hd, :], scalar=t_eb[:, hd, c * L + L - 1:c * L + L],
                    in1=bvk[:, hd, :], op0=ALU.mult, op1=ALU.add), reads=[cur[1], Bt_eb, Bb], writes=[nxt[1]])
            if chain and (full or need_shadow):
                S.op("act", lambda e, nxt=nxt, nxtb=nxtb: e.activation(out=nxtb[0], in_=nxt[0], func=AF.Copy),
                     reads=[nxt[1]], writes=[nxtb[1]])
            if on_state is not None:
                on_state(c, nxt)
        yield
        if not full:
            return
        yield
        bk, Bb = hrot.get()
        yield
        proj_tm(bk, Bb, Wh, 0, 512, hTa, T, [BWh, BhTa])
        yield
        S.op("act", lambda e: e.activation(out=t_er[:T], in_=bk[:T, :], func=AF.Silu), reads=[Bb], writes=[Bt_er])
        yield
        S.op("dve", lambda e: e.scalar_tensor_tensor(out=t_qb[:T], in0=t_er[:T], scalar=float(128 ** -0.5),
                                                     in1=t_f[:T], op0=ALU.mult, op1=ALU.mult),
             reads=[Bt_f, Bt_er], writes=[Bt_qb])
        yield
        for src_, Bsrc_, dst_, Bdst_ in ((t_qb, Bt_qb, t_q, Bt_q), (t_x, Bt_x, t_sg, Bt_sg)):
            bk, Bb = hrot.get(); bvt = bk.bitcast(BF16).rearrange("p (a b) -> p a b", a=8)
            S.group("pe", [tr(bvt[:, hd, :T], src_[:T, hd * 128:(hd + 1) * 128], identb[:T, :T]) for hd in range(4)],
                    reads=[Bsrc_, Bones], writes=[Bb])
            S.op("act", lambda e, bvt=bvt, dst_=dst_: e.activation(out=dst_[:, :, :T], in_=bvt[:, 0:4, :T], func=AF.Copy),
                 reads=[Bb], writes=[Bdst_])
        yield
        bk, Bb = hrot.get(); bva = bk.rearrange("p (a b) -> p a b", a=4)
        yield
        S.group("pe", [mm(bva[:T, hd, :T], t_sg[:, hd, :T], t_q[:, hd, :T], True, True) for hd in range(4)],
                reads=[Bt_sg, Bt_q], writes=[Bb])
        yield
        S.op("dve", lambda e: e.tensor_tensor(out=t_am[:T, :, :T], in0=bva[:T, :, :T],
                                              in1=BTm[:T, :T].unsqueeze(1).to_broadcast([T, 4, T]), op=ALU.mult),
             reads=[Bb, Bc], writes=[Bt_am])
        yield
        obk, Bo = orot.get(); obv = obk.rearrange("p (a b) -> p a b", a=4)
        fns = []
        yield
        for hd in range(4):
            fns.append(mm(obv[:, hd, :T], t_v[:T, hd * 128:(hd + 1) * 128], t_am[:T, hd, :T], hd == 0, False,
                          skip_group_check=True))
        yield
        for hd in range(4):
            for c in range(4):
                fns.append(mm(obv[:, hd, c * L:(c + 1) * L], slots[c][0][:, hd, :], t_q[:, hd, c * L:(c + 1) * L],
                              False, (hd == 3 and c == 3), skip_group_check=True))
        yield
        S.group("pe", fns, reads=[Bt_v, Bt_am, Bt_q] + [sl_[1] for sl_ in slots], writes=[Bo])
        yield
        S.op("act", lambda e: e.activation(out=t_osq[:, :, :T], in_=obv[:, :, :T], func=AF.Square), reads=[Bo], writes=[Bt_osq])
        yield
        bk, Bb = hrot.get(); bvm = bk.rearrange("p (a b) -> p a b", a=4)
        yield
        S.group("pe", [mm(bvm[:, hd, :T], onesb, t_osq[:, hd, :T], True, True) for hd in range(4)],
                reads=[Bones, Bt_osq], writes=[Bb])
        yield
        S.op("act", lambda e: e.activation(out=t_rs[:, :, :T], in_=bvm[:, :, :T], func=AF.Ln, scale=1.0 / 128, bias=EPS),
             reads=[Bb], writes=[Bt_rs])
        yield
        S.op("act", lambda e: e.activation(out=t_rs[:, :, :T], in_=t_rs[:, :, :T], func=AF.Exp, scale=-0.5),
             reads=[Bt_rs], writes=[Bt_rs])
        yield
        S.op("dve", lambda e: e.tensor_tensor(out=t_rs[:, :, :T], in0=obv[:, :, :T], in1=t_rs[:, :, :T], op=ALU.mult),
             reads=[Bo, Bt_rs], writes=[Bt_rs])
        yield
        bk, Bb = hrot.get()
        yield
        proj_tm(bk, Bb, Wh, 1536, 512, hTa, T, [BWh, BhTa])
        yield
        S.op("act", lambda e: e.activation(out=t_x[:T], in_=bk[:T, :], func=AF.Silu), reads=[Bb], writes=[Bt_x])
        yield
        bk, Bb = hrot.get(); bvg = bk.bitcast(BF16).rearrange("p (a b) -> p a b", a=8)
        yield
        S.group("pe", [tr(bvg[:, hd, :T], t_x[:T, hd * 128:(hd + 1) * 128], identb[:T, :T]) for hd in range(4)],
                reads=[Bt_x, Bones], writes=[Bb])
        yield
        S.op("act", lambda e: e.activation(out=t_sgg[:, :, :T], in_=bvg[:, 0:4, :T], func=AF.Copy), reads=[Bb], writes=[Bt_sgg])
        yield
        for hd in range(4):
            S.op("dve", lambda e, hd=hd: e.scalar_tensor_tensor(
                out=mixdst[:, hd, :T], in0=t_rs[:, hd, :T], scalar=gcols[:, 36 + hd:37 + hd], in1=t_sgg[:, hd, :T],
                op0=ALU.mult, op1=ALU.mult), reads=[Bt_rs, Bt_sgg, Bc], writes=[Bmixdst])

    def wout_tile(T, xa, Bx, mA, BmA, mH, BmH):
        for hf in range(2):
            bk, Bb = brot.get()
            fns = []
            for k in range(8):
                src = mA[:, k, :T] if k < 4 else mH[:, k - 4, :T]
                fns.append(mm(bk[:T, :], src, Wout[:, k, hf * 512:(hf + 1) * 512], k == 0, k == 7))
            S.group("pe", fns, reads=[BmA, BmH, BWh], writes=[Bb])
            S.op("dve", lambda e, bk=bk, hf=hf: e.tensor_tensor(out=xa[:T, hf * 512:(hf + 1) * 512], in0=bk[:T, :],
                                                               in1=xa[:T, hf * 512:(hf + 1) * 512], op=ALU.add),
                 reads=[Bb], writes=[Bx])

    def wco_tile(T, xa, Bx):
        for hf in range(2):
            bk, Bb = brot.get()
            S.group("pe", [mm(bk[:T, :], coT[:, k, :T], Wco[:, k, hf * 512:(hf + 1) * 512], k == 0, k == 3)
                           for k in range(4)], reads=[BcoT, BWh], writes=[Bb])
            S.op("dve", lambda e, bk=bk, hf=hf: e.tensor_tensor(out=xa[:T, hf * 512:(hf + 1) * 512], in0=bk[:T, :],
                                                               in1=xa[:T, hf * 512:(hf + 1) * 512], op=ALU.add),
                 reads=[Bb], writes=[Bx])

    def cross_prompt(T):
        bk, Bb = brot.get()
        yield
        proj_tm(bk, Bb, Wcq, 0, 512, cT, T, [BWh, BcT])
        yield
        S.op("act", lambda e: e.activation(out=t_xb[:T], in_=bk[:T, :], func=AF.Copy), reads=[Bb], writes=[Bt_xb])
        yield
        bk, Bb = brot.get(); bvq = bk.bitcast(BF16).rearrange("p (a b) -> p a b", a=8)
        yield
        S.group("pe", [tr(bvq[:, hd, :T], t_xb[:T, hd * 128:(hd + 1) * 128], identb[:T, :T]) for hd in range(4)],
                reads=[Bt_xb, Bones], writes=[Bb])
        yield
        S.op("act", lambda e: e.activation(out=cqT[:, :, :T], in_=bvq[:, 0:4, :T], func=AF.Copy), reads=[Bb], writes=[BcqT])
        yield
        for mb in range(2):
            bk, Bb = brot.get(); bvs = bk.rearrange("p (a b) -> p a b", a=4)
            S.group("pe", [mm(bvs[:, hd, :T], MKT[:, hd, mb * 128:(mb + 1) * 128], cqT[:, hd, :T], True, True)
                           for hd in range(4)], reads=[BMKT, BcqT], writes=[Bb])
            S.op("act", lambda e, bvs=bvs, mb=mb: e.activation(out=Pc[mb][0][:, :, :T], in_=bvs[:, :, :T], func=AF.Exp,
                                                              scale=float(128 ** -0.5)),
                 reads=[Bb], writes=[Pc[mb][1]])
        yield
        obk, Bo = obrot.get(); obv = obk.rearrange("p (a b) -> p a b", a=4)
        fns = []
        for hd in range(4):
            for mb in range(2):
                fns.append(mm(obv[:, hd, :T], MV[:, mb, hd * 128:(hd + 1) * 128], Pc[mb][0][:, hd, :T], mb == 0, mb == 1))
        yield
        S.group("pe", fns, reads=[BMV, Pc[0][1], Pc[1][1]], writes=[Bo])
        yield
        bk, Bb = brot.get(); bvd = bk.rearrange("p (a b) -> p a b", a=4)
        fns = []
        for hd in range(4):
            for mb in range(2):
                fns.append(mm(bvd[:, hd, :T], onesb, Pc[mb][0][:, hd, :T], mb == 0, mb == 1))
        yield
        S.group("pe", fns, reads=[Bones, Pc[0][1], Pc[1][1]], writes=[Bb])
        yield
        rdv = rden.rearrange("p (a b) -> p a b", a=4)
        yield
        S.op("act", lambda e: e.activation(out=rdv[:, :, :T], in_=bvd[:, :, :T], func=AF.Ln), reads=[Bb], writes=[Brden])
        S.op("act", lambda e: e.activation(out=rdv[:, :, :T], in_=rdv[:, :, :T], func=AF.Exp, scale=-1.0),
             reads=[Brden], writes=[Brden])
        yield
        S.op("dve", lambda e: e.tensor_tensor(out=coT[:, :, :T], in0=obv[:, :, :T], in1=rdv[:, :, :T], op=ALU.mult),
             reads=[Bo, Brden], writes=[BcoT])

    def run_threads(gens):
        gens = [g for g in gens if g is not None]
        while gens:
            for g in list(gens):
                try:
                    next(g)
                except StopIteration:
                    gens.remove(g)

    def thrA(ti):
        sl = ti % 3
        main = ti >= T0
        if ti + 1 < NT:
            S.dma("sp", xt[(ti + 1) % 3], xc[(ti + 1) * 128:(ti + 2) * 128, :], writes=[Bxt[(ti + 1) % 3]],
                  sem="xH%d" % ((ti + 1) % 3))
        norm_T(xt[sl], Bxt[sl], 128, 0, hT[ti % 2], BhT[ti % 2], rot=hrot)
        yield
        mt = ti - T0
        yield from hgrn_tile(128, 32, hT[ti % 2], BhT[ti % 2], (btp, m2p, cmp_), main, True,
                             mixH[:, :, mt * 128:(mt + 1) * 128] if main else None, BmixH[mt] if main else None,
                             need_shadow=(ti == T0 - 1))

    def thrB(ti):
        sl = ti % 3
        mt = ti - T0
        wout_tile(128, xt[sl], Bxt[sl], mixT[:, :, mt * 128:(mt + 1) * 128], BmixA[mt],
                  mixH[:, :, mt * 128:(mt + 1) * 128], BmixH[mt])
        yield
        norm_T(xt[sl], Bxt[sl], 128, 8, cT, BcT, rot=brot)
        yield
        yield from cross_prompt(128)
        wco_tile(128, xt[sl], Bxt[sl])
        yield
        S.dma("pool", x2d[mt * 128:(mt + 1) * 128, :], xt[sl], reads=[Bxt[sl]], writes=[Bx2d[mt]], sem="x2w%d" % sl)

    S.dma("sp", xt[0], xc[0:128, :], writes=[Bxt[0]], sem="xH0")
    pend = None
    for ti in range(NT):
        run_threads([thrA(ti), pend])
        pend = thrB(ti) if ti >= T0 else None
    run_threads([pend])
    fin = ring[rcur[0] % NR]
    S.dma("pool", o_hs.rearrange("h k v -> k h v"), fin[0], reads=[fin[1]], sem="ohs")

    xst = xt[0][:NS]; Bxst = Bxt[0]
    S0all = [ring[(rcur[0] + 1 + i) % NR] for i in range(NSQ)]
    S.dma("sp", xst, xs, writes=[Bxst], sem="xsH")
    for i in range(NSQ):
        S.dma("sp", S0all[i][0], shg[i].rearrange("h k v -> k h v"), writes=[S0all[i][1]], sem="s0_%d" % i)
    norm_T(xst, Bxst, NS, 0, hsT, BhsT)

    def emit_state(c, nxt):
        S.dma("pool", o_hss[c].rearrange("h k v -> k h v"), nxt[0], reads=[nxt[1]], sem="ohss%d" % (c % 2))

    for _ in hgrn_tile(NS, 4, hsT, BhsT, (bts, m2s, cms), True, False, mixHs, BmixsH, s0list=S0all,
                       sout=[Sos[c % 2] for c in range(NSQ)], on_state=emit_state):
        pass
    wout_tile(NS, xst, Bxst, mixTs, BmixsA, mixHs, BmixsH)
    norm_T(xst, Bxst, NS, 8, cT, BcT)
    if debug:
        d_og = nc.dram_tensor("d_og", [128, 4, NMAIN * 128], BF16, kind="ExternalOutput").ap()
        d_ogs = nc.dram_tensor("d_ogs", [128, 4, NS], BF16, kind="ExternalOutput").ap()
        S.dma("sp", d_og, mixH, reads=BmixH, sem="dbg")
        S.dma("sp", d_ogs, mixHs, reads=[BmixsH], sem="dbg")
    S.barrier()
    esHw.close()
    esHs = contextlib.ExitStack()
    selb = sbt(esHs, "selbH", (16, 16, 128)); Bsel = Buf()
    S.dma("sp", selb, c_selb, writes=[Bsel], sem="cHs"); S.seal([Bsel], "cHs")
    cqs = sbt(esHs, "cqs", (NS, 512)); Bcqs = Buf()
    MKs = [sbt(esHs, "MKs%d" % i, (128, 2, 512)) for i in range(2)]; BMKs = [Buf(), Buf()]
    MVs = [sbt(esHs, "MVs%d" % i, (128, 2, 4, 129)) for i in range(2)]; BMVs = [Buf(), Buf()]
    for i in range(2):
        S.op("pool", lambda e, i=i: e.memset(MVs[i], 1.0), writes=[BMVs[i]])
    prodc = sbt(esHs, "prodc", (128, 2, 512)); Bprodc = Buf()
    SCc = sbt(esHs, "SCc", (128, 2, 4, 4)); BSCc = Buf()
    PCz = [sbt(esHs, "PCz%d" % i, (128, 2, NS, 4)) for i in range(NSQ)]; BPCz = [Buf() for _ in range(NSQ)]
    for i in range(NSQ):
        S.op("pool", lambda e, i=i: e.memset(PCz[i], 0.0), writes=[BPCz[i]])
    bk, Bb = hrot.get()
    proj_tm(bk, Bb, Wcq, 0, 512, cT, NS, [BWh, BcT])
    S.op("act", lambda e, bk=bk: e.activation(out=cqs, in_=bk[:NS, :], func=AF.Copy), reads=[Bb], writes=[Bcqs])
    obc = [banks[6][:NS, 0:258].rearrange("p (h d) -> p h d", h=2), banks[7][:NS, 0:258].rearrange("p (h d) -> p h d", h=2)]
    Bobc = [Bbank[6], Bbank[7]]
    firstc = [True, True]
    for s_ in range(NSQ):
        sl = s_ % 2
        S.dma("sp", MKs[sl], cmk[s_].rearrange("(b p) c -> p b c", p=128), writes=[BMKs[sl]], sem="mk%d" % sl)
        for blk in range(2):
            S.dma("sp", MVs[sl][:, blk, :, 0:128],
                  cmv[s_, blk * 128:(blk + 1) * 128, :].rearrange("p (h d) -> p h d", h=4), writes=[BMVs[sl]],
                  sem="mv%d" % sl)
        for i in range(4):
            tq = 4 * s_ + i
            bk, Bb = hrot.get()
            S.group("pe", [mm(bk[:, :], selb[:, tq, :], cqs[:, 0:512], True, True)], reads=[Bsel, Bcqs], writes=[Bb])
            S.op("dve", lambda e, bk=bk, sl=sl: e.tensor_tensor(
                out=prodc, in0=MKs[sl], in1=bk.unsqueeze(1).to_broadcast([128, 2, 512]), op=ALU.mult),
                reads=[BMKs[sl], Bb], writes=[Bprodc])
            S.op("dve", lambda e, i=i: e.tensor_reduce(
                out=SCc[:, :, i, :], in_=prodc.rearrange("p b (h d) -> p b h d", h=4), axis=AX.X, op=ALU.add),
                reads=[Bprodc], writes=[BSCc])
        S.op("act", lambda e, s_=s_: e.activation(out=PCz[s_][:, :, 4 * s_:4 * s_ + 4, :], in_=SCc, func=AF.Exp,
                                                  scale=float(128 ** -0.5)), reads=[BSCc], writes=[BPCz[s_]])
        for blk in range(2):
            for g in range(2):
                fns = []
                for hh in range(2):
                    fns.append(mm(obc[g][:, hh, :], PCz[s_][:, blk, :, 2 * g + hh], MVs[sl][:, blk, 2 * g + hh, :],
                                  firstc[g] and hh == 0, False, skip_group_check=True))
                firstc[g] = False
                S.group("pe", fns, reads=[BPCz[s_], BMVs[sl]], writes=[Bobc[g]])
    attn_finish(NS, obc, Bobc, 4, 128, None, None, None)
    tm_to_fm(NS, on_sb, Bon, 4, None, coT, BcoT)
    wco_tile(NS, xst, Bxst)
    S.dma("sp", x2d[NMAIN * 128:NMAIN * 128 + NS, :], xst, reads=[Bxst], writes=[Bx2d[NMAIN]], sem="x2ws")
    S.barrier()
    esHs.close()
    esH.close()

    if False:
        d_og = nc.dram_tensor("d_og", [128, 4, NMAIN * 128], BF16, kind="ExternalOutput").ap()
        d_ogs = nc.dram_tensor("d_ogs", [128, 4, NS], BF16, kind="ExternalOutput").ap()
        S.dma("sp", d_og, mixH, reads=BmixH, sem="dbg")
        S.dma("sp", d_ogs, mixHs, reads=[BmixsH], sem="dbg")
    if stop_after == "H0":
        S.finish()
        return nc, S

    esF = contextlib.ExitStack()
    hrot = Rot([0, 1, 2, 3, 4, 5])
    wrot.ids = [0, 1, 2, 3, 4, 5]
    orot = Rot([6, 7])
    gfin = sbt(esF, "gfin", (128, D)); BcF = Buf()
    S.dma("sp", gfin, norm_final.rearrange("(o d) -> o d", o=1).partition_broadcast(128)[:, 0, :], writes=[BcF], sem="cF")
    S.seal([BcF], "cF")
    Wd = sbt(esF, "Wd", (128, NF, D), BF16); BWd = Buf()
    wdv = w_down.rearrange("(f p) c -> p f c", p=128)
    for f in range(NF):
        S.dma("pool", Wd[:, f, :], wdv[:, f, :], writes=[BWd], sem="wd")
    S.seal([BWd], "wd")
    uT = sbt(esF, "uT", (128, 8, 512), BF16); BuT = [Buf() for _ in range(4)]
    usT = sbt(esF, "usT", (128, 8, NS), BF16); BusT = Buf()
    uhT = sbt(esF, "uhT", (128, 8, 128), BF16); BuhT = Buf()
    xt2 = [sbt(esF, "xt2_%d" % i, (128, D)) for i in range(4)]; Bxt2 = [Buf() for _ in range(4)]
    xs2 = sbt(esF, "xs2", (NS, D)); Bxs2 = Buf()
    aT = sbt(esF, "aT", (128, NF, 512), BF16); BaT = [Buf() for _ in range(NF)]
    aTs = sbt(esF, "aTs", (128, NF, NS), BF16); BaTs = Buf()
    NWS = 3
    Wg = [sbt(esF, "Wg%d" % i, (128, 8, 256), BF16) for i in range(NWS)]
    Wu = [sbt(esF, "Wu%d" % i, (128, 8, 256), BF16) for i in range(NWS)]
    BWs = [Buf() for _ in range(NWS)]
    ubs = [sbt(esF, "ubs%d" % i, (128, 512), BF16) for i in range(2)]; Bubs = [Buf(), Buf()]
    sil = [sbt(esF, "sil%d" % i, (128, 512), BF16) for i in range(2)]; Bsil = [Buf(), Buf()]
    acc = [sbt(esF, "acc%d" % i, (128, 512)) for i in range(2)]; Bacc = [Buf(), Buf()]
    halo = sbt(esF, "halo", (128, NF, 2)); Bhalo = [Buf() for _ in range(NF)]
    bufT = sbt(esF, "bufT", (128, NF, 8)); BbufT = Buf()
    sfct = sbt(esF, "sfct", (8, DFF)); Bsfct = Buf()
    gexs = sbt(esF, "gexs", (128, 4, 6)); Bgexs = Buf()
    accs = sbt(esF, "accs", (128, 4, 4)); Baccs = Buf()
    fst = [sbt(esF, "fst%d" % i, (NS, 128)) for i in range(2)]; Bfst = [Buf(), Buf()]
    fsp = [sbt(esF, "fsp%d" % i, (2, 512)) for i in range(2)]; Bfsp = [Buf(), Buf()]
    ucat = sbt(esF, "ucat", (128, 8, NS + 2), BF16); Bucat = Buf()
    gsc = sbt(esF, "gsc", (128, NS)); Bgsc = Buf()
    wgv = w_gate.rearrange("(k p) c -> p k c", p=128)
    wuv = w_up.rearrange("(k p) c -> p k c", p=128)
    NG = NF // 2
    NBLK = 4

    def wstream(gi):
        fg = gi % NG
        sl = gi % NWS
        S.dma("pool", Wg[sl], wgv[:, :, fg * 256:(fg + 1) * 256], writes=[BWs[sl]], sem="ws%d" % sl)
        S.dma("pool", Wu[sl], wuv[:, :, fg * 256:(fg + 1) * 256], writes=[BWs[sl]], sem="ws%d" % sl)

    wstream(0); wstream(1)
    S.dma("sp", xt2[0], x2d[0:128, :], reads=[Bx2d[0]], writes=[Bxt2[0]], sem="xF0")
    norm_T(xt2[0], Bxt2[0], 128, 16, uhT, BuhT)
    S.dma("sp", xs2, x2d[NMAIN * 128:NMAIN * 128 + NS, :], reads=[Bx2d[NMAIN]], writes=[Bxs2], sem="xFs")
    norm_T(xs2, Bxs2, NS, 16, usT, BusT)
    S.op("pool", lambda e: e.tensor_copy(out=ucat[:, :, 0:NS], in_=usT), reads=[BusT], writes=[Bucat])
    S.op("pool", lambda e: e.tensor_copy(out=ucat[:, :, NS:NS + 2], in_=uhT[:, :, 126:128]), reads=[BuhT], writes=[Bucat])
    S.dma("sp", sfct, sfc, writes=[Bsfct], sem="sfct")
    for f0 in range(0, NF, 4):
        nf_ = min(4, NF - f0)
        bk, Bb = hrot.get(); bv = bk.rearrange("p (a b) -> p a b", a=4)
        S.group("pe", [tr(bv[:, c, :8], sfct[:8, (f0 + c) * 128:(f0 + c + 1) * 128], ident[:8, :8]) for c in range(nf_)],
                reads=[Bsfct, Bc], writes=[Bb])
        S.op("act", lambda e, bv=bv, f0=f0, nf_=nf_: e.activation(out=bufT[:, f0:f0 + nf_, :], in_=bv[:, 0:nf_, :8],
                                                               func=AF.Copy), reads=[Bb], writes=[BbufT])
    nfs = [0]
    gi = 0
    for bi in range(NBLK):
        for r in range(4):
            mt = 1 + 4 * bi + r
            S.dma("sp", xt2[r], x2d[mt * 128:(mt + 1) * 128, :], reads=[Bx2d[mt]], writes=[Bxt2[r]], sem="xF%d" % r)
            norm_T(xt2[r], Bxt2[r], 128, 16, uT[:, :, r * 128:(r + 1) * 128], BuT[r])
        for fg in range(NG):
            if gi + 2 < NBLK * NG:
                wstream(gi + 2)
            sl = gi % NWS
            gi += 1
            for f2 in range(2):
                f = 2 * fg + f2
                e_ = f % 2
                wg = lambda k: Wg[sl][:, k, f2 * 128:(f2 + 1) * 128]
                wu = lambda k: Wu[sl][:, k, f2 * 128:(f2 + 1) * 128]
                gb, Bg = hrot.get()
                S.group("pe", [mm(gb[:, :], wg(k), uT[:, k, :], k == 0, k == 7) for k in range(8)],
                        reads=[BWs[sl]] + BuT, writes=[Bg])
                ub, Bu = hrot.get()
                S.group("pe", [mm(ub[:, :], wu(k), uT[:, k, :], k == 0, k == 7) for k in range(8)],
                        reads=[BWs[sl]] + BuT, writes=[Bu])
                if bi == 0:
                    xb, Bx = hrot.get()
                    fns = [mm(xb[:, 0:NS + 2], wg(k), ucat[:, k, :], k == 0, k == 7) for k in range(8)]
                    fns += [mm(xb[:, 32:32 + NS], wu(k), usT[:, k, :], k == 0, k == 7) for k in range(8)]
                    S.group("pe", fns, reads=[BWs[sl], BusT, Bucat], writes=[Bx])
                    S.op("dve", lambda e, xb=xb, f=f: e.tensor_scalar(out=halo[:, f, :], in0=xb[:, NS:NS + 2],
                                                                      scalar1=hfl[:, 0:1], scalar2=None, op0=ALU.mult),
                         reads=[Bx, Bc], writes=[Bhalo[f]])
                    S.op("dve", lambda e, xb=xb: e.tensor_copy(out=gsc, in_=xb[:, 0:NS]), reads=[Bx], writes=[Bgsc])
                    xb2, Bx2 = hrot.get()
                    S.group("pe", [tr(xb2[:NS, 0:128], gsc[:, :NS], ident)], reads=[Bgsc, Bc], writes=[Bx2])
                    so = nfs[0] % 2; nfs[0] += 1
                    S.op("dve", lambda e, xb2=xb2, so=so: e.tensor_copy(out=fst[so], in_=xb2[:NS, 0:128]),
                         reads=[Bx2], writes=[Bfst[so]])
                    S.dma("sp", o_fcs[:, f * 128:(f + 1) * 128], fst[so], reads=[Bfst[so]], sem="ofs%d" % so)
                    S.op("pool", lambda e, f=f: e.tensor_copy(out=gexs[:, :, 0:2],
                                                              in_=bufT[:, f, :].rearrange("p (s j) -> p s j", s=4)),
                         reads=[BbufT], writes=[Bgexs])
                    S.op("dve", lambda e, xb=xb: e.tensor_copy(out=gexs[:, :, 2:6],
                                                               in_=xb[:, 0:NS].rearrange("p (s i) -> p s i", s=4)),
                         reads=[Bx], writes=[Bgexs])
                    S.op("dve", lambda e, f=f: e.tensor_scalar(out=accs, in0=gexs[:, :, 2:6], scalar1=cwc[:, f, 2:3],
                                                               scalar2=cbc[:, f:f + 1], op0=ALU.mult, op1=ALU.add),
                         reads=[Bgexs, Bc], writes=[Baccs])
                    for j_ in (1, 0):
                        S.op("dve", lambda e, f=f, j_=j_: e.scalar_tensor_tensor(
                            out=accs, in0=gexs[:, :, j_:j_ + 4], scalar=cwc[:, f, j_:j_ + 1], in1=accs,
                            op0=ALU.mult, op1=ALU.add), reads=[Bgexs, Bc], writes=[Baccs])
                    S.op("act", lambda e: e.activation(out=accs, in_=accs, func=AF.Silu), reads=[Baccs], writes=[Baccs])
                    S.op("dve", lambda e, xb=xb, f=f: e.tensor_tensor(
                        out=aTs[:, f, :].rearrange("p (s i) -> p s i", s=4), in0=accs,
                        in1=xb[:, 32:32 + NS].rearrange("p (s i) -> p s i", s=4), op=ALU.mult),
                        reads=[Baccs, Bx], writes=[BaTs])
                S.op("act", lambda e, e_=e_, f=f, gb=gb: e.activation(out=acc[e_], in_=gb, func=AF.Identity,
                                                                    scale=cwc[:, f, 2:3], bias=cbc[:, f:f + 1]),
                     reads=[Bg, Bc], writes=[Bacc[e_]])
                S.op("act", lambda e, e_=e_, ub=ub: e.activation(out=ubs[e_], in_=ub, func=AF.Copy),
                     reads=[Bu], writes=[Bubs[e_]])
                S.op("dve", lambda e, e_=e_, f=f, gb=gb: e.scalar_tensor_tensor(
                    out=acc[e_][:, 1:512], in0=gb[:, 0:511], scalar=cwc[:, f, 1:2], in1=acc[e_][:, 1:512],
                    op0=ALU.mult, op1=ALU.add), reads=[Bg, Bc], writes=[Bacc[e_]])
                S.op("dve", lambda e, e_=e_, f=f, gb=gb: e.scalar_tensor_tensor(
                    out=acc[e_][:, 2:512], in0=gb[:, 0:510], scalar=cwc[:, f, 0:1], in1=acc[e_][:, 2:512],
                    op0=ALU.mult, op1=ALU.add), reads=[Bg, Bc], writes=[Bacc[e_]])
                S.op("dve", lambda e, e_=e_, f=f: e.scalar_tensor_tensor(
                    out=acc[e_][:, 0:2], in0=halo[:, f, 0:2], scalar=cwc[:, f, 0:1], in1=acc[e_][:, 0:2],
                    op0=ALU.mult, op1=ALU.add), reads=[Bhalo[f], Bc], writes=[Bacc[e_]])
                S.op("dve", lambda e, e_=e_, f=f: e.scalar_tensor_tensor(
                    out=acc[e_][:, 0:1], in0=halo[:, f, 1:2], scalar=cwc[:, f, 1:2], in1=acc[e_][:, 0:1],
                    op0=ALU.mult, op1=ALU.add), reads=[Bhalo[f], Bc], writes=[Bacc[e_]])
                S.op("dve", lambda e, f=f, gb=gb: e.tensor_copy(out=halo[:, f, :], in_=gb[:, 510:512]),
                     reads=[Bg], writes=[Bhalo[f]])
                S.op("act", lambda e, e_=e_: e.activation(out=sil[e_], in_=acc[e_], func=AF.Silu),
                     reads=[Bacc[e_]], writes=[Bsil[e_]])
                S.op("pool", lambda e, e_=e_, f=f: e.tensor_tensor(out=aT[:, f, :], in0=sil[e_], in1=ubs[e_],
                                                                   op=ALU.mult),
                     reads=[Bsil[e_], Bubs[e_]], writes=[BaT[f]])
        if bi == NBLK - 1:
            for f0 in range(0, NF, 4):
                nf_ = min(4, NF - f0)
                xb, Bx = hrot.get()
                S.group("pe", [tr(xb[:2, c * 128:(c + 1) * 128], halo[:, f0 + c, :], ident) for c in range(nf_)],
                        reads=[Bhalo[f0 + c] for c in range(nf_)] + [Bc], writes=[Bx])
                so = (f0 // 4) % 2
                S.op("dve", lambda e, xb=xb, so=so, nf_=nf_: e.tensor_copy(out=fsp[so][:, 0:nf_ * 128], in_=xb[:2, 0:nf_ * 128]),
                     reads=[Bx], writes=[Bfsp[so]])
                S.dma("sp", o_fc[:, f0 * 128:(f0 + nf_) * 128], fsp[so][:, 0:nf_ * 128], reads=[Bfsp[so]], sem="ofp%d" % so)
        def down_tile(T, aTa, BaTa, xa, Bx, dst, dsem):
            for hf in range(2):
                ob_, Bo_ = orot.get()
                S.group("pe", [mm(ob_[:T, :], aTa(f), Wd[:, f, hf * 512:(hf + 1) * 512], f == 0, f == NF - 1)
                               for f in range(NF)], reads=[BWd] + BaTa, writes=[Bo_])
                S.op("dve", lambda e, ob_=ob_, hf=hf: e.tensor_tensor(out=xa[:T, hf * 512:(hf + 1) * 512], in0=ob_[:T, :],
                                                                     in1=xa[:T, hf * 512:(hf + 1) * 512], op=ALU.add),
                     reads=[Bo_], writes=[Bx])
            S.op("act", lambda e: e.activation(out=junk[:T], in_=xa[:T], func=AF.Square, accum_out=stat[:T, 0:1]),
                 reads=[Bx], writes=[Bjunk, Bstat])
            rstd_from_ss(T, slice(0, 1), slice(1, 2), float(D))
            S.op("dve", lambda e: e.scalar_tensor_tensor(out=xa[:T], in0=xa[:T], scalar=stat[:T, 1:2], in1=gfin[:T],
                                                         op0=ALU.mult, op1=ALU.mult), reads=[Bstat, BcF], writes=[Bx])
            S.dma("sp", dst, xa[:T], reads=[Bx], sem=dsem)

        for r in range(4):
            down_tile(128, lambda f, r=r: aT[:, f, r * 128:(r + 1) * 128], BaT, xt2[r], Bxt2[r],
                      o_y[(4 * bi + r) * 128:(4 * bi + r + 1) * 128, :], "oy%d" % r)
        if bi == 0:
            down_tile(NS, lambda f: aTs[:, f, :], [BaTs], xs2, Bxs2, o_ys, "oys")
    S.barrier()
    esF.close()
    S.finish()
    return nc, S


_NC_CACHE = {}


def _prep(inp):
    cs = _consts()
    f = lambda a: np.ascontiguousarray(np.asarray(a, dtype=np.float32))
    xp = f(inp["x_prompt"]); xsm = f(inp["x_sample"])
    shared = {
        "hg_lb_logits": f(inp["hg_lb_logits"]), "norm_mix": f(inp["norm_mix"])[0], "w_in": f(inp["w_in"])[0],
        "att_out_norm": f(inp["att_out_norm"])[0], "hg_out_norm": f(inp["hg_out_norm"])[0],
        "w_out": f(inp["w_out"])[0], "norm_cross": f(inp["norm_cross"])[0], "norm_mem": f(inp["norm_mem"])[0],
        "w_cq": f(inp["w_cq"])[0], "w_ck": f(inp["w_ck"])[0], "w_cv": f(inp["w_cv"])[0], "w_co": f(inp["w_co"])[0],
        "norm_ffn": f(inp["norm_ffn"])[0], "w_gate": f(inp["w_gate"])[0], "w_up": f(inp["w_up"])[0],
        "conv_w": f(inp["conv_w"])[0], "conv_b": f(inp["conv_b"])[0], "w_down": f(inp["w_down"])[0],
        "norm_final": f(inp["norm_final"]),
    }
    shared.update(cs)
    maps = []
    for c in range(8):
        b, half = c // 2, c % 2
        m = dict(shared)
        xcore = np.zeros((NT * 128, D), np.float32)
        vf = np.zeros((NT * 128,), np.float32)
        if half == 1:
            xcore[:] = xp[b]
            vf[:] = 1.0
        else:
            xcore[2048:] = xp[b, :2048]
            vf[2048:] = 1.0
        m["xc"] = xcore
        m["vflag"] = np.ascontiguousarray(vf.reshape(NT, 128).T)
        m["hflag"] = np.full((128, 1), float(half), np.float32)
        sq = slice(4 * c, 4 * c + 4)
        m["xs"] = np.ascontiguousarray(xsm[sq].reshape(NS, D))
        m["cwk"] = np.ascontiguousarray(f(inp["cache_win_k"])[0, sq].reshape(NSQ, 2048, 512))
        m["cwv"] = np.ascontiguousarray(f(inp["cache_win_v"])[0, sq].reshape(NSQ, 2048, 512))
        m["shg"] = np.ascontiguousarray(f(inp["state_hgrn"])[0, sq])
        m["sfc"] = np.ascontiguousarray(f(inp["state_ffn_conv"])[0, sq].reshape(NSQ * 2, DFF))
        m["cmk"] = np.ascontiguousarray(f(inp["cache_mem_k"])[0, sq].reshape(NSQ, 256, 512))
        m["cmv"] = np.ascontiguousarray(f(inp["cache_mem_v"])[0, sq].reshape(NSQ, 256, 512))
        m["memp"] = np.ascontiguousarray(f(inp["mem_prompt"])[b])
        maps.append(m)
    return maps


def _assemble(res):
    R = res.results
    yp = np.zeros((4, 4096, D), np.float32)
    ys = np.zeros((32, 4, D), np.float32)
    wk = np.zeros((1, 4, 2048, 8, 64), np.float32); wv = np.zeros_like(wk)
    hs = np.zeros((1, 4, 4, 128, 128), np.float32)
    fc = np.zeros((1, 4, 2, DFF), np.float32)
    mk = np.zeros((1, 4, 256, 4, 128), np.float32); mv = np.zeros_like(mk)
    wks = np.zeros((1, 32, 4, 8, 64), np.float32); wvs = np.zeros_like(wks)
    hss = np.zeros((1, 32, 4, 128, 128), np.float32)
    fcs = np.zeros((1, 32, 2, DFF), np.float32)
    for c in range(8):
        b, half = c // 2, c % 2
        r = R[c]
        yp[b, half * 2048:(half + 1) * 2048] = r["o_y"]
        sq = slice(4 * c, 4 * c + 4)
        ys[sq] = r["o_ys"].reshape(4, 4, D)
        wks[0, sq] = r["o_wks"].reshape(4, 4, 8, 64)
        wvs[0, sq] = r["o_wvs"].reshape(4, 4, 8, 64)
        hss[0, sq] = r["o_hss"]
        fcs[0, sq] = r["o_fcs"].reshape(4, 4, DFF)[:, 2:4]
        if half == 1:
            wk[0, b] = r["o_wk"].reshape(2048, 8, 64)
            wv[0, b] = r["o_wv"].reshape(2048, 8, 64)
            hs[0, b] = r["o_hs"]
            fc[0, b] = r["o_fc"]
            mk[0, b] = r["o_mk"].reshape(256, 4, 128)
            mv[0, b] = r["o_mv"].reshape(256, 4, 128)
    return (yp, ys, wk, wv, hs, fc, mk, mv, wks, wvs, hss, fcs)


def kernel(**inputs):
    if "nc" not in _NC_CACHE:
        _NC_CACHE["nc"] = build_nc()[0]
    nc = _NC_CACHE["nc"]
    maps = _prep(inputs)
    res = run_bass_kernel_spmd(nc, maps, core_ids=list(range(8)))
    return _assemble(res)
```
